# Optimizing a Trainium2 kernel written in Bass

```python
import jax, jax.numpy as jnp
from jax import lax
import numpy as np

D_MODEL = 2048
BATCH = 32
SEQ = 256
DEPTH = 4
DEC_BATCH = 8
DEC_SEQ = 4096
PAST_LEN = 512

GRID_W = 64
N_MIXERS = 3
N_A_LAYERS = (DEPTH + 2) // 3
N_B_LAYERS = (DEPTH + 1) // 3
N_C_LAYERS = DEPTH // 3
CHUNK = 128
A_WIDTH = D_MODEL
A_GROUPS = 8
A_GROUP_DIM = A_WIDTH // A_GROUPS
HEAD_DIM = 128
N_Q_HEADS = D_MODEL // HEAD_DIM
N_KV_HEADS = 4
Q_PER_KV = N_Q_HEADS // N_KV_HEADS
Q_BLOCK = 128
ROPE_THETA = 10000.0
ROPE_AXIS_DIM = HEAD_DIM // 2
POOL_WINDOWS = (2, 4, 8, 16)
POOL_GROUP_DIM = D_MODEL // len(POOL_WINDOWS)
FFN_HIDDEN = -(-8 * D_MODEL // (3 * 256)) * 256
N_MOD = 6
EPS = 1e-6

kernel_name = 'hybrid_diffusion_prefix_trunk_step'


def _rms(x, g):
    xf = x.astype(jnp.float32)
    y = xf * lax.rsqrt(jnp.mean(xf * xf, axis=-1, keepdims=True) + EPS)
    return (y * g.astype(jnp.float32)).astype(x.dtype)


def _swiglu(h, w_gu, w_down):
    gu = h @ w_gu
    g, u = gu[..., :FFN_HIDDEN], gu[..., FFN_HIDDEN:]
    return (jax.nn.silu(g) * u) @ w_down


def _chunk_gmlp(h, w_in, g_v, w_s, b_s, w_out):
    B, T, _ = h.shape
    a = h @ w_in
    u, v = a[..., :A_WIDTH], a[..., A_WIDTH:]
    v = _rms(v, g_v).reshape(B, T // CHUNK, CHUNK, A_GROUPS, A_GROUP_DIM)
    s = jnp.einsum('gpq,bnqge->bnpge', w_s, v) + b_s.T[None, None, :, :, None]
    return (u * s.reshape(B, T, A_WIDTH)) @ w_out


def _pool_mixer(h, w_pool, scale):
    B, T, D = h.shape
    hf = h.astype(jnp.float32)
    cs = jnp.concatenate([jnp.zeros((B, 1, D), jnp.float32), jnp.cumsum(hf, axis=1)], axis=1)
    t = jnp.arange(T)
    outs = []
    for j, w in enumerate(POOL_WINDOWS):
        lo = jnp.clip(t - w // 2, 0, T)
        hi = jnp.clip(t + w // 2, 0, T)
        sl = slice(j * POOL_GROUP_DIM, (j + 1) * POOL_GROUP_DIM)
        cnt = (hi - lo).astype(jnp.float32)[None, :, None]
        outs.append((cs[:, hi, sl] - cs[:, lo, sl]) / cnt - hf[:, :, sl])
    p = jnp.stack(outs, axis=2).astype(h.dtype)
    y = jnp.einsum('btgc,gcd->btgd', p, w_pool).reshape(B, T, D)
    return y * scale


def _rope_2d(x):
    T = x.shape[1]
    t = jnp.arange(T)
    n_rows = T // GRID_W
    row = jnp.minimum(t // GRID_W, n_rows - 1).astype(jnp.float32)
    col = (t % GRID_W).astype(jnp.float32)
    inv = jnp.power(ROPE_THETA, -jnp.arange(0, ROPE_AXIS_DIM, 2, dtype=jnp.float32) / ROPE_AXIS_DIM)
    ang = jnp.concatenate([row[:, None] * inv, col[:, None] * inv], axis=-1)
    ang = ang.reshape((1, T) + (1,) * (x.ndim - 3) + (HEAD_DIM // 2,))
    cos, sin = jnp.cos(ang), jnp.sin(ang)
    xr = x.astype(jnp.float32).reshape(x.shape[:-1] + (HEAD_DIM // 2, 2))
    x1, x2 = xr[..., 0], xr[..., 1]
    out = jnp.stack([x1 * cos - x2 * sin, x1 * sin + x2 * cos], axis=-1)
    return out.reshape(x.shape).astype(x.dtype)


def _qkv(h, w_qkv, q_norm, k_norm):
    B, T, _ = h.shape
    qkv = h @ w_qkv
    nq, nk = N_Q_HEADS * HEAD_DIM, N_KV_HEADS * HEAD_DIM
    q = qkv[..., :nq].reshape(B, T, N_KV_HEADS, Q_PER_KV, HEAD_DIM)
    k = qkv[..., nq:nq + nk].reshape(B, T, N_KV_HEADS, HEAD_DIM)
    v = qkv[..., nq + nk:].reshape(B, T, N_KV_HEADS, HEAD_DIM)
    return _rms(q, q_norm), _rms(k, k_norm), v


def _attend(q, k, v):
    B, T = q.shape[:2]
    nb = T // Q_BLOCK
    qb = q.reshape(B, nb, Q_BLOCK, N_KV_HEADS, Q_PER_KV, HEAD_DIM).swapaxes(0, 1)
    scale = HEAD_DIM ** -0.5

    def one(qi):
        s = jnp.einsum('bqhgd,bkhd->bhgqk', qi, k).astype(jnp.float32) * scale
        p = jax.nn.softmax(s, axis=-1).astype(v.dtype)
        return jnp.einsum('bhgqk,bkhd->bqhgd', p, v)

    o = lax.map(one, qb)
    return o.swapaxes(0, 1).reshape(B, T, N_Q_HEADS * HEAD_DIM)


def _trunk(x, cond, cache_k, cache_v, p):
    latent = cache_k is not None
    Bc = cond.shape[0]
    ia = ib = ic = 0
    new_k, new_v = [], []
    for l in range(DEPTH):
        mod = (jax.nn.silu(cond) @ p['w_mod'][l] + p['b_mod'][l]).reshape(Bc, 1, N_MOD, D_MODEL)
        h = _rms(x, p['norm_mix_pre'][l]) * (1 + mod[:, :, 1]) + mod[:, :, 0]
        kind = l % N_MIXERS
        if kind == 0:
            m = _chunk_gmlp(h, p['a_w_in'][ia], p['a_norm_v'][ia], p['a_w_s'][ia], p['a_b_s'][ia], p['a_w_out'][ia])
            ia += 1
        elif kind == 1:
            q, k, v = _qkv(h, p['b_w_qkv'][ib], p['b_q_norm'][ib], p['b_k_norm'][ib])
            if latent:
                q, k = _rope_2d(q), _rope_2d(k)
                k_all = jnp.concatenate([k, cache_k[:, ib].astype(k.dtype)], axis=1)
                v_all = jnp.concatenate([v, cache_v[:, ib].astype(v.dtype)], axis=1)
                o = _attend(q, k_all, v_all)
            else:
                new_k.append(k)
                new_v.append(v)
                o = _attend(q, k, v)
            m = o @ p['b_w_o'][ib]
            ib += 1
        else:
            m = _pool_mixer(h, p['c_w_pool'][ic], p['c_scale'][ic])
            ic += 1
        x = x + mod[:, :, 2] * _rms(m, p['norm_mix_post'][l])
        h = _rms(x, p['norm_ffn_pre'][l]) * (1 + mod[:, :, 4]) + mod[:, :, 3]
        f = _swiglu(h, p['f_w_gu'][l], p['f_w_down'][l])
        x = x + mod[:, :, 5] * _rms(f, p['norm_ffn_post'][l])
    if latent:
        return x, None, None
    return x, jnp.stack(new_k, axis=1), jnp.stack(new_v, axis=1)


def setup_inputs(seed: int = 0) -> dict:
    key = jax.random.key(seed)
    ks = jax.random.split(key, 26)
    n = lambda k, shape, s: jax.random.normal(k, shape, jnp.float32) * s
    D = D_MODEL
    qkv_out = (N_Q_HEADS + 2 * N_KV_HEADS) * HEAD_DIM
    kv_shape = (DEC_BATCH, N_B_LAYERS, PAST_LEN, N_KV_HEADS, HEAD_DIM)
    return {
        'x_prompt': n(ks[0], (BATCH, SEQ, D), 1.0),
        'x_sample': n(ks[1], (DEC_BATCH, DEC_SEQ, D), 1.0),
        'cache_k': n(ks[2], kv_shape, 1.0),
        'cache_v': n(ks[3], kv_shape, 1.0),
        'c': n(ks[4], (DEC_BATCH, D), 1.0),
        'c_ctx': n(ks[5], (D,), 1.0),
        'w_mod': n(ks[6], (DEPTH, D, N_MOD * D), D ** -0.5),
        'b_mod': n(ks[7], (DEPTH, N_MOD * D), 0.01),
        'norm_mix_pre': 1.0 + n(ks[8], (DEPTH, D), 0.02),
        'norm_mix_post': 1.0 + n(ks[9], (DEPTH, D), 0.02),
        'norm_ffn_pre': 1.0 + n(ks[10], (DEPTH, D), 0.02),
        'norm_ffn_post': 1.0 + n(ks[11], (DEPTH, D), 0.02),
        'a_w_in': n(ks[12], (N_A_LAYERS, D, 2 * A_WIDTH), D ** -0.5),
        'a_norm_v': 1.0 + n(ks[13], (N_A_LAYERS, A_WIDTH), 0.02),
        'a_w_s': n(ks[14], (N_A_LAYERS, A_GROUPS, CHUNK, CHUNK), CHUNK ** -0.5),
        'a_b_s': 1.0 + n(ks[15], (N_A_LAYERS, A_GROUPS, CHUNK), 0.02),
        'a_w_out': n(ks[16], (N_A_LAYERS, A_WIDTH, D), A_WIDTH ** -0.5),
        'b_w_qkv': n(ks[17], (N_B_LAYERS, D, qkv_out), D ** -0.5),
        'b_q_norm': 1.0 + n(ks[18], (N_B_LAYERS, HEAD_DIM), 0.02),
        'b_k_norm': 1.0 + n(ks[19], (N_B_LAYERS, HEAD_DIM), 0.02),
        'b_w_o': n(ks[20], (N_B_LAYERS, N_Q_HEADS * HEAD_DIM, D), (N_Q_HEADS * HEAD_DIM) ** -0.5),
        'c_w_pool': n(ks[21], (N_C_LAYERS, len(POOL_WINDOWS), POOL_GROUP_DIM, POOL_GROUP_DIM), POOL_GROUP_DIM ** -0.5),
        'c_scale': 1.0 + n(ks[22], (N_C_LAYERS, D), 0.1),
        'f_w_gu': n(ks[23], (DEPTH, D, 2 * FFN_HIDDEN), D ** -0.5),
        'f_w_down': n(ks[24], (DEPTH, FFN_HIDDEN, D), FFN_HIDDEN ** -0.5),
    }


def reference(x_prompt, x_sample, cache_k, cache_v, c, c_ctx, w_mod, b_mod,
              norm_mix_pre, norm_mix_post, norm_ffn_pre, norm_ffn_post,
              a_w_in, a_norm_v, a_w_s, a_b_s, a_w_out,
              b_w_qkv, b_q_norm, b_k_norm, b_w_o,
              c_w_pool, c_scale, f_w_gu, f_w_down):
    p = {
        'w_mod': w_mod, 'b_mod': b_mod,
        'norm_mix_pre': norm_mix_pre, 'norm_mix_post': norm_mix_post,
        'norm_ffn_pre': norm_ffn_pre, 'norm_ffn_post': norm_ffn_post,
        'a_w_in': a_w_in, 'a_norm_v': a_norm_v, 'a_w_s': a_w_s, 'a_b_s': a_b_s, 'a_w_out': a_w_out,
        'b_w_qkv': b_w_qkv, 'b_q_norm': b_q_norm, 'b_k_norm': b_k_norm, 'b_w_o': b_w_o,
        'c_w_pool': c_w_pool, 'c_scale': c_scale, 'f_w_gu': f_w_gu, 'f_w_down': f_w_down,
    }
    y_prompt, state_k, state_v = _trunk(x_prompt, c_ctx[None, :], None, None, p)
    y_sample, _, _ = _trunk(x_sample, c, cache_k, cache_v, p)
    return (y_prompt, y_sample, state_k, state_v)
```

```python
from contextlib import ExitStack
import numpy as np
import concourse.bass as bass
import concourse.mybir as mybir
from concourse.bass_utils import run_bass_kernel_spmd

F32 = mybir.dt.float32
BF16 = mybir.dt.bfloat16
AF = mybir.ActivationFunctionType
ALU = mybir.AluOpType

D = 2048
NCH = 16
DEPTH = 4
SEQ_S = 4096
NTOK = 5120
HID = 5632
NHC = 44
EPS = 1e-6
NDMA = 8
NWSLOT = 3
ENGS = ["pe", "act", "dve", "pool", "sp"]


class Prog:
    def __init__(self, nc, es):
        self.nc = nc
        self.sem = {}
        names = ["pe", "act", "dve", "pool"]
        for q_ in ("sp", "pool", "act", "bg"):
            names += ["d%s%d" % (q_, i) for i in range(NDMA)]
        for n in names:
            self.sem[n] = es.enter_context(nc.semaphore("s_" + n))
        self.dma_i = {"sp": 0, "pool": 0, "act": 0, "bg": 0}
        self.cnt = {n: 0 for n in self.sem}
        self.known = {e: {} for e in ENGS}
        self.lastw = {}
        self.reads = {}
        self.q = {e: [] for e in ENGS}
        self.nops = 0

    def op(self, eng, fn, r=(), w=(), dma=False):
        waits = {}

        def need(t):
            if t is None:
                return
            s, v = t
            if v > 0 and waits.get(s, 0) < v:
                waits[s] = v

        for k in r:
            need(self.lastw.get(k))
        for k in w:
            need(self.lastw.get(k))
            for s, v in self.reads.get(k, {}).items():
                need((s, v))
        if dma:
            qn = "bg" if dma == "bg" else eng
            s = "d%s%d" % (qn, self.dma_i[qn] % NDMA)
            self.dma_i[qn] += 1
            need((s, self.cnt[s]))
            self.cnt[s] += 16
            amt = 16
        else:
            s = eng
            self.cnt[s] += 1
            amt = 1
        ticket = (s, self.cnt[s])
        wl = []
        kn = self.known[eng]
        for s2, v in waits.items():
            if kn.get(s2, 0) < v:
                wl.append((s2, v))
                kn[s2] = v
        self.q[eng].append((fn, wl, s, amt))
        for k in r:
            self.reads.setdefault(k, {})[ticket[0]] = ticket[1]
        for k in w:
            self.lastw[k] = ticket
            self.reads[k] = {}
        self.nops += 1
        return ticket

    def flush(self, final=False):
        nc = self.nc
        q = self.q
        sem = self.sem
        cnt = self.cnt

        def run(e, name):
            for fn, wl, s, amt in q[name]:
                for s2, v in wl:
                    e.wait_ge(sem[s2], v)
                ins = fn(e)
                ins.then_inc(sem[s], amt)
            for s2, v in cnt.items():
                if v > 0:
                    e.wait_ge(sem[s2], v)

        with nc.Block() as blk:
            @blk.tensor
            def _(e):
                run(e, "pe")

            @blk.scalar
            def _(e):
                run(e, "act")

            @blk.vector
            def _(e):
                run(e, "dve")

            @blk.gpsimd
            def _(e):
                run(e, "pool")

            @blk.sync
            def _(e):
                run(e, "sp")
        self.q = {e: [] for e in ENGS}
        for e in ENGS:
            self.known[e] = dict(cnt)


def mm_group(out, pairs, start=True):
    def fn(pe):
        n = len(pairs)
        ins = None
        for i, (a, b) in enumerate(pairs):
            ins = pe.matmul(out, a, b, start=(start and i == 0), stop=(i == n - 1))
        return ins
    return fn


class K:
    pass


def weight_units():
    units = []
    idx = {}

    def add(key, name, l, k0, nK, col0):
        idx[key] = len(units)
        units.append((name, l, k0, nK, col0))

    def ffn(l):
        for j in range(NHC // 2):
            add(("g", l, j), "f_w_gu", l, 0, 16, j * 256)
            add(("u", l, j), "f_w_gu", l, 0, 16, HID + j * 256)
        for q in range(4):
            for cp in range(8):
                add(("d", l, q, cp), "f_w_down", l, q * 11, 11, cp * 256)

    def gm(a):
        for j in range(8):
            add(("au", a, j), "a_w_in", a, 0, 16, j * 256)
        for j in range(8):
            add(("av", a, j), "a_w_in", a, 0, 16, 2048 + j * 256)
        for j in range(8):
            add(("ao", a, j), "a_w_out", a, 0, 16, j * 256)

    gm(0)
    ffn(0)
    for j in range(12):
        add(("qkv", j), "b_w_qkv", 0, 0, 16, j * 256)
    for j in range(8):
        add(("bo", j), "b_w_o", 0, 0, 16, j * 256)
    ffn(1)
    for g in range(4):
        for j in range(2):
            add(("cp", g, j), "c_w_pool", g, 0, 4, j * 256)
    ffn(2)
    gm(1)
    ffn(3)
    return units, idx


def build(stop_after=99, dbg=False):
    nc = bass.Bass("TRN2", target_bir_lowering=False)
    es = ExitStack()
    with es:
        return _build(nc, es, stop_after, dbg)


def _build(nc, es, stop_after, dbg):
    def din(name, shape):
        return nc.dram_tensor(name, list(shape), F32, kind="ExternalInput").ap()

    def dout(name, shape):
        return nc.dram_tensor(name, list(shape), F32, kind="ExternalOutput").ap()

    I = {}
    I["xs"] = din("xs", [SEQ_S, D])
    I["xp"] = din("xp", [1024, D])
    I["ck"] = din("ck", [512, 512])
    I["cv"] = din("cv", [512, 512])
    I["cond2"] = din("cond2", [2, D])
    I["w_mod"] = din("w_mod", [DEPTH, D, 6 * D])
    I["b_mod"] = din("b_mod", [DEPTH, 6 * D])
    for n in ["norm_mix_pre", "norm_mix_post", "norm_ffn_pre", "norm_ffn_post"]:
        I[n] = din(n, [DEPTH, D])
    I["a_w_in"] = din("a_w_in", [2, D, 2 * D])
    I["a_norm_v"] = din("a_norm_v", [2, D])
    I["a_w_s"] = din("a_w_s", [2, 8, 128, 128])
    I["a_b_s"] = din("a_b_s", [2, 8, 128])
    I["a_w_out"] = din("a_w_out", [2, D, D])
    I["b_w_qkv"] = din("b_w_qkv", [1, D, 3072])
    I["b_q_norm"] = din("b_q_norm", [1, 128])
    I["b_k_norm"] = din("b_k_norm", [1, 128])
    I["b_w_o"] = din("b_w_o", [1, D, D])
    I["c_w_pool"] = din("c_w_pool", [4, 512, 512])
    I["c_scale"] = din("c_scale", [1, D])
    I["f_w_gu"] = din("f_w_gu", [DEPTH, D, 2 * HID])
    I["f_w_down"] = din("f_w_down", [DEPTH, HID, D])
    I["ident"] = din("ident", [128, 128])
    I["rmat"] = din("rmat", [128, 128])
    I["cs_tab"] = din("cs_tab", [16, 128, 1024])
    I["pool_inv"] = din("pool_inv", [128, 64])

    O = {}
    O["yp"] = dout("yp", [1024, D])
    O["ys"] = dout("ys", [SEQ_S, D])
    O["sk"] = dout("sk", [1024, 512])
    O["sv"] = dout("sv", [1024, 512])
    if dbg:
        O["dbg"] = dout("dbg", [128, NCH * NTOK])

    units, uidx = weight_units()
    NU = len(units)
    wb_parts = [nc.dram_tensor("wb%d" % i, [128, 128, 4096], BF16).ap() for i in range((NU + 127) // 128)]

    class _WB:
        def __getitem__(self, key):
            u = key[0]
            return wb_parts[u // 128][(u % 128,) + tuple(key[1:])]
    wb = _WB()
    xscr = [nc.dram_tensor("xscr%d" % i, [128, NCH * NTOK], F32).ap().rearrange("p (c t) -> p c t", t=NTOK)
            for i in range(2)]

    P = Prog(nc, es)
    k = K()
    k.nc, k.P, k.I, k.O, k.wb, k.xscr, k.uidx, k.units = nc, P, I, O, wb, xscr, uidx, units

    def sb(name, shape, dt=F32):
        return es.enter_context(nc.sbuf_tensor("sb_" + name, list(shape), dt))

    k.ident = sb("ident", [128, 128])
    k.identb = sb("identb", [128, 128], BF16)
    k.rmat = sb("rmatb", [128, 128], BF16)
    k.onesD = sb("onesD", [128, 128], BF16)
    k.onesH = sb("onesH", [128, 128], BF16)
    k.ones1 = sb("ones1", [128, 128], BF16)
    k.epsb = sb("epsb", [128, 1])
    k.modT = sb("modT", [128, DEPTH * 6 * NCH * 2])
    k.gT = sb("gT", [128, 4 * DEPTH * NCH])
    k.misc = sb("miscT", [128, 128])
    k.AB = sb("AB", [128, DEPTH * 2 * 2 * 3 * NCH])
    k.pinv = sb("pinv", [128, 64])

    with ExitStack() as es0:
        def sb0(name, shape, dt=F32):
            return es0.enter_context(nc.sbuf_tensor("s0_" + name, list(shape), dt))

        def ps0(name):
            return es0.enter_context(nc.psum_tensor(name, [128, 512], F32))

        stgA = sb0("stgA", [128, 128])
        stgB = sb0("stgB", [128, 128])
        stgC = sb0("stgC", [128, 128])
        stgM = [sb0("stgM%d" % i, [128, 128]) for i in range(3)]
        bmT = sb0("bmT", [128, 384])
        rm32 = sb0("rm32", [128, 128])
        scT = sb0("scT", [128, 32], BF16)
        wm = [sb0("wm%d" % i, [128, 16 * 768], BF16) for i in range(2)]
        pst = [ps0("pst%d" % i) for i in range(2)]
        psm = [ps0("psm%d" % i) for i in range(4)]

        P.op("sp", lambda e: e.dma_start(out=k.ident[:], in_=I["ident"][:, :]), w=["ident"], dma=True)
        P.op("sp", lambda e: e.dma_start(out=rm32[:], in_=I["rmat"][:, :]), w=["rm32"], dma=True)
        P.op("sp", lambda e: e.dma_start(out=k.pinv[:], in_=I["pool_inv"][:, :]), w=["pinv"], dma=True)
        P.op("dve", lambda e: e.memset(k.onesD[:], 1.0 / D), w=["onesD"])
        P.op("dve", lambda e: e.memset(k.onesH[:], 1.0 / 128), w=["onesH"])
        P.op("dve", lambda e: e.memset(k.ones1[:], 1.0), w=["ones1"])
        P.op("dve", lambda e: e.memset(k.epsb[:], EPS), w=["epsb"])
        P.op("dve", lambda e: e.tensor_copy(k.identb[:], k.ident[:]), r=["ident"], w=["identb"])
        P.op("dve", lambda e: e.tensor_copy(k.rmat[:], rm32[:]), r=["rm32"], w=["rmat"])
        for st_, nm_ in ((stgA, "stgA"), (stgB, "stgB"), (stgC, "stgC")):
            P.op("pool", lambda e, st_=st_: e.memset(st_[:], 0.0), w=[nm_])

        def rows(name):
            return I[name].rearrange("l (c f) -> (l c) f", f=128)

        P.op("sp", lambda e: e.dma_start(out=stgA[0:64, :], in_=rows("norm_mix_pre")), w=["stgA"], dma=True)
        P.op("sp", lambda e: e.dma_start(out=stgA[64:128, :], in_=rows("norm_mix_post")), w=["stgA"], dma=True)
        P.op("sp", lambda e: e.dma_start(out=stgB[0:64, :], in_=rows("norm_ffn_pre")), w=["stgB"], dma=True)
        P.op("sp", lambda e: e.dma_start(out=stgB[64:128, :], in_=rows("norm_ffn_post")), w=["stgB"], dma=True)
        P.op("sp", lambda e: e.dma_start(out=stgC[0:32, :], in_=rows("a_norm_v")), w=["stgC"], dma=True)
        P.op("sp", lambda e: e.dma_start(out=stgC[32:48, :], in_=rows("c_scale")), w=["stgC"], dma=True)
        P.op("sp", lambda e: e.dma_start(out=stgC[48:80, :], in_=rows("cond2")), w=["stgC"], dma=True)
        P.op("sp", lambda e: e.dma_start(out=stgC[80:81, :], in_=I["b_q_norm"][:, :]), w=["stgC"], dma=True)
        P.op("sp", lambda e: e.dma_start(out=stgC[81:82, :], in_=I["b_k_norm"][:, :]), w=["stgC"], dma=True)
        bm_rows = I["b_mod"].rearrange("l (c f) -> (l c) f", f=128)
        for i in range(3):
            P.op("sp", lambda e, i=i: e.dma_start(out=stgM[i][:], in_=bm_rows[i * 128:(i + 1) * 128, :]),
                 w=["stgM%d" % i], dma=True)

        def tr(dst, src, skey, dkey, pi):
            P.op("pe", lambda e: e.transpose(pst[pi][:, 0:128], src[:], k.ident[:]), r=[skey, "ident"], w=["pst%d" % pi])
            P.op("dve", lambda e: e.tensor_copy(dst, pst[pi][:, 0:128]), r=["pst%d" % pi], w=[dkey])

        tr(k.gT[:, 0:128], stgA, "stgA", "gT", 0)
        tr(k.gT[:, 128:256], stgB, "stgB", "gT", 1)
        tr(k.misc[:, :], stgC, "stgC", "misc", 0)
        for i in range(3):
            tr(bmT[:, i * 128:(i + 1) * 128], stgM[i], "stgM%d" % i, "bmT", (i + 1) % 2)
        P.op("act", lambda e: e.activation(out=scT[:].rearrange("p (c i) -> p i c", i=2),
                                           in_=k.misc[:, 48:80].rearrange("p (i c) -> p i c", i=2), func=AF.Silu),
             r=["misc"], w=["scT"])
        modv = k.modT[:].rearrange("p (l m c i) -> p l m c i", l=DEPTH, m=6, c=NCH)
        for l in range(DEPTH):
            wsrc = I["w_mod"][l].rearrange("(kc p) n -> p kc n", p=128)
            for pc in range(16):
                s = (l * 16 + pc) % 2
                wmv = wm[s][:].rearrange("p (kc n) -> p kc n", n=768)
                P.op("pool", lambda e, wmv=wmv, wsrc=wsrc, pc=pc: e.dma_start(out=wmv, in_=wsrc[:, :, pc * 768:(pc + 1) * 768]),
                     w=["wm%d" % s], dma=True)
                for ch in range(6):
                    gch = pc * 6 + ch
                    pairs = [(wmv[:, kc, ch * 128:(ch + 1) * 128], scT[:, kc * 2:kc * 2 + 2]) for kc in range(16)]
                    P.op("pe", mm_group(psm[l][:, gch * 2:gch * 2 + 2], pairs), r=["wm%d" % s, "scT"], w=["psm%d" % l])
            for ci in range(2):
                P.op("dve", lambda e, l=l, ci=ci: e.tensor_tensor(
                    k.modT[:, l * 192:(l + 1) * 192].rearrange("p (g i) -> p g i", i=2)[:, :, ci],
                    psm[l][:, 0:192].rearrange("p (g i) -> p g i", i=2)[:, :, ci],
                    bmT[:, l * 96:(l + 1) * 96], ALU.add), r=["psm%d" % l, "bmT"], w=["modT"])
        ABv = k.AB[:].rearrange("p (l s i q c) -> p l s i q c", l=DEPTH, s=2, i=2, q=3)
        gv = k.gT[:].rearrange("p (q l c) -> p q l c", q=4, l=DEPTH)
        for l in range(DEPTH):
            for sub in range(2):
                for ci in range(2):
                    mo = 3 * sub
                    P.op("dve", lambda e, l=l, sub=sub, ci=ci, mo=mo: e.scalar_tensor_tensor(
                        ABv[:, l, sub, ci, 0, :], modv[:, l, mo + 1, :, ci], 1.0, gv[:, 2 * sub, l, :], ALU.add, ALU.mult),
                        r=["modT", "gT"], w=["AB"])
                    P.op("dve", lambda e, l=l, sub=sub, ci=ci, mo=mo: e.tensor_copy(
                        ABv[:, l, sub, ci, 1, :], modv[:, l, mo, :, ci]), r=["modT"], w=["AB"])
                    P.op("dve", lambda e, l=l, sub=sub, ci=ci, mo=mo: e.tensor_tensor(
                        ABv[:, l, sub, ci, 2, :], modv[:, l, mo + 2, :, ci], gv[:, 2 * sub + 1, l, :], ALU.mult),
                        r=["modT", "gT"], w=["AB"])
        k.ABv = ABv

        k.conv_next = 0
        for _ in range(24):
            conv_step(k)
        P.flush()

    arena = es.enter_context(nc.sbuf_tensor("arena", [128, AR], F32))
    k.arena = arena
    k.wslot = [sb("wslot%d" % i, [128, 4096], BF16) for i in range(NWSLOT)]
    k.sq = sb("sq", [128, 4 * 512], BF16)
    k.rstd = sb("rstd", [128, 2 * 512])
    k.tmp = sb("tmp", [128, NTMP * 512])
    k.gm_ws = sb("gm_ws", [128, 1024])
    k.gm_bs = sb("gm_bs", [128, 1024])
    k.gm_wsr = sb("gm_wsr", [128, 4 * 1024], BF16)
    k.sm = sb("smallf", [128, 64])
    k.psall = es.enter_context(nc.psum_tensor("psall", [128, 4096], F32))
    k.psum = [k.psall[:, i * 512:(i + 1) * 512] for i in range(8)]
    k.ps_i = 0
    k.w_i = 0
    k.sq_i = 0
    k.tmp_i = 0

    run_layers(k, stop_after, dbg)
    P.flush(final=True)
    return nc


AR = 34304
NTMP = 6


def fview(k, off, n, T):
    return k.arena[:, off:off + n * T].rearrange("p (c t) -> p c t", t=T)


def bview(k, off, n, T):
    return k.arena[:, off:off + n * T // 2].bitcast(BF16).rearrange("p (c t) -> p c t", t=T)


def conv_step(k):
    u = k.conv_next
    if u >= len(k.units):
        return
    k.conv_next += 1
    name, l, k0, nK, col0 = k.units[u]
    src = k.I[name][l].rearrange("(kc p) n -> p kc n", p=128)[:, k0:k0 + nK, col0:col0 + 256]
    dst = k.wb[u, :, 0:nK * 256].rearrange("p (kc n) -> p kc n", n=256)
    k.P.op("pool", lambda e: e.dma_start(out=dst, in_=src), w=[("wb", u)], dma="bg")


def next_ps(k):
    i = k.ps_i % getattr(k, "nps", 7)
    k.ps_i += 1
    return k.psum[i], "ps%d" % i


def wload(k, key, nK):
    P = k.P
    u = k.uidx[key]
    s = k.w_i % NWSLOT
    k.w_i += 1
    slot = k.wslot[s]
    if k.w_i % 3 == 0:
        conv_step(k)
    while k.conv_next <= u:
        conv_step(k)
    P.op("sp", lambda e: e.dma_start(out=slot[:, 0:nK * 256], in_=k.wb[u, :, 0:nK * 256]),
         r=[("wb", u)], w=["wslot%d" % s], dma=True)
    return slot[:, 0:nK * 256].rearrange("p (kc n) -> p kc n", n=256), "wslot%d" % s


def rms_stats(k, src, skey, n, T, rs_slot, ones, bank=7):
    P = k.P
    sqv = k.sq[:].rearrange("p (s t) -> p s t", t=512)
    stb = k.psum[bank]
    bkey = "ps%d" % bank
    for c in range(n):
        s = k.sq_i % 4
        k.sq_i += 1
        P.op("act", lambda e, c=c, s=s: e.activation(out=sqv[:, s, 0:T], in_=src(c), func=AF.Square),
             r=[skey(c)], w=["sq%d" % s])
        P.op("pe", lambda e, c=c, s=s: e.matmul(stb[:, 0:T], ones[:], sqv[:, s, 0:T], start=(c == 0), stop=(c == n - 1)),
             r=["sq%d" % s], w=[bkey])
    rs = k.rstd[:, rs_slot * 512:rs_slot * 512 + T]
    if getattr(k, "lnexp", False):
        P.op("act", lambda e: e.activation(out=rs, in_=stb[:, 0:T], func=AF.Ln, bias=k.epsb[:, 0:1], scale=1.0),
             r=[bkey], w=["rstd%d" % rs_slot])
        P.op("act", lambda e: e.activation(out=rs, in_=rs, func=AF.Exp, scale=-0.5), r=["rstd%d" % rs_slot], w=["rstd%d" % rs_slot])
    else:
        P.op("act", lambda e: e.activation(out=rs, in_=stb[:, 0:T], func=AF.Sqrt, bias=k.epsb[:, 0:1], scale=1.0),
             r=[bkey], w=["rstd%d" % rs_slot])
        P.op("dve", lambda e: e.reciprocal(rs, rs), r=["rstd%d" % rs_slot], w=["rstd%d" % rs_slot])
    return rs, "rstd%d" % rs_slot


def tmp_slot(k, T):
    s = k.tmp_i % NTMP
    k.tmp_i += 1
    return k.tmp[:, s * 512:s * 512 + T], "tmp%d" % s


def prenorm_stats(k, X, xkey, T):
    return rms_stats(k, lambda c: X[:, c, 0:T], lambda c: (xkey, c), NCH, T, 0, k.onesD)


def prenorm_apply(k, X, xkey, T, l, sub, ci, out, okey, rs, rkey):
    P = k.P
    for c in range(NCH):
        t, tk = tmp_slot(k, T)
        P.op("dve", lambda e, c=c, t=t: e.tensor_tensor(t, X[:, c, 0:T], rs, ALU.mult), r=[(xkey, c), rkey], w=[tk])
        P.op("act", lambda e, c=c, t=t: e.activation(out=out(c), in_=t, func=AF.Identity,
                                                     bias=k.ABv[:, l, sub, ci, 1, c:c + 1], scale=k.ABv[:, l, sub, ci, 0, c:c + 1]),
             r=[tk], w=[okey(c)])


def prenorm(k, X, xkey, T, l, sub, ci, out, okey):
    rs, rkey = prenorm_stats(k, X, xkey, T)
    prenorm_apply(k, X, xkey, T, l, sub, ci, out, okey, rs, rkey)


def postnorm_residual(k, F, fkey, X, xkey, xoff, T, l, sub, ci, bank=7):
    P = k.P
    rs, rkey = rms_stats(k, lambda c: F[:, c, 0:T], lambda c: (fkey, c), NCH, T, 1, k.onesD, bank=bank)
    for c in range(NCH):
        t, tk = tmp_slot(k, T)
        P.op("dve", lambda e, c=c, t=t: e.scalar_tensor_tensor(t, F[:, c, 0:T], k.ABv[:, l, sub, ci, 2, c:c + 1], rs, ALU.mult, ALU.mult),
             r=[(fkey, c), rkey], w=[tk])
        P.op("pool", lambda e, c=c, t=t: e.tensor_tensor(F[:, c, 0:T], t, X[:, c, xoff:xoff + T], ALU.add),
             r=[tk, (xkey, c)], w=[(fkey, c)])


def load_x(k, src_buf, tok0, T, X, xkey, col0=0):
    k.P.op("pool", lambda e: e.dma_start(out=X[:, :, col0:col0 + T], in_=k.xscr[src_buf][:, :, tok0:tok0 + T]),
           w=[(xkey, c) for c in range(NCH)], dma=True)


def store_x(k, dst_buf, tok0, T, F, fkey):
    k.P.op("pool", lambda e: e.dma_start(out=k.xscr[dst_buf][:, :, tok0:tok0 + T], in_=F[:, :, 0:T]),
           r=[(fkey, c) for c in range(NCH)], dma=True)


def barrier(k):
    k.P.flush()
    k.P.lastw = {}
    k.P.reads = {}


def ingest_pass(k):
    P = k.P
    T = 512
    XT = [fview(k, 0, 16, T), fview(k, 8192, 16, T)]
    stage = [k.arena[:, 16384 + i * 2048:16384 + (i + 1) * 2048] for i in range(3)]
    si = 0
    for i in range(NTOK // T):
        X, xkey = XT[i % 2], "xt%d" % (i % 2)
        for j in range(4):
            t0 = i * T + j * 128
            src = k.I["xs"][t0:t0 + 128, :] if t0 < SEQ_S else k.I["xp"][t0 - SEQ_S:t0 - SEQ_S + 128, :]
            s = si % 3
            si += 1
            stg = stage[s]
            P.op("sp", lambda e, stg=stg, src=src: e.dma_start(out=stg, in_=src), w=["stage%d" % s], dma=True)
            for c4 in range(4):
                ps, pk = next_ps(k)

                def fn(e, stg=stg, ps=ps, c4=c4):
                    ins = None
                    for ii in range(4):
                        c = c4 * 4 + ii
                        ins = e.transpose(ps[:, ii * 128:(ii + 1) * 128], stg[:, c * 128:(c + 1) * 128], k.ident[:])
                    return ins
                P.op("pe", fn, r=["stage%d" % s], w=[pk])
                dst = X[:, c4 * 4:c4 * 4 + 4, j * 128:(j + 1) * 128]
                srcv = ps[:, :].rearrange("p (c t) -> p c t", t=128)
                keys = [(xkey, c4 * 4 + ii) for ii in range(4)]
                if c4 % 2 == 0:
                    P.op("dve", lambda e, dst=dst, srcv=srcv: e.tensor_copy(dst, srcv), r=[pk], w=keys)
                else:
                    P.op("act", lambda e, dst=dst, srcv=srcv: e.activation(out=dst, in_=srcv, func=AF.Copy), r=[pk], w=keys)
        store_x(k, 0, i * T, T, X, xkey)


def emit_pass(k, src_buf):
    P = k.P
    T = 512
    XT = [fview(k, 0, 16, T), fview(k, 8192, 16, T)]
    stage = [k.arena[:, 16384 + i * 2048:16384 + (i + 1) * 2048] for i in range(3)]
    si = 0
    for i in range(NTOK // T):
        X, xkey = XT[i % 2], "xt%d" % (i % 2)
        load_x(k, src_buf, i * T, T, X, xkey)
        for j in range(4):
            t0 = i * T + j * 128
            dst = k.O["ys"][t0:t0 + 128, :] if t0 < SEQ_S else k.O["yp"][t0 - SEQ_S:t0 - SEQ_S + 128, :]
            s = si % 3
            si += 1
            stg = stage[s]
            for c4 in range(4):
                ps, pk = next_ps(k)

                def fn(e, ps=ps, c4=c4, j=j, X=X):
                    ins = None
                    for ii in range(4):
                        c = c4 * 4 + ii
                        ins = e.transpose(ps[:, ii * 128:(ii + 1) * 128], X[:, c, j * 128:(j + 1) * 128], k.ident[:])
                    return ins
                P.op("pe", fn, r=[(xkey, c4 * 4 + ii) for ii in range(4)], w=[pk])
                if c4 % 2 == 0:
                    P.op("dve", lambda e, stg=stg, ps=ps, c4=c4: e.tensor_copy(stg[:, c4 * 512:(c4 + 1) * 512], ps[:, :]), r=[pk], w=["stage%d" % s])
                else:
                    P.op("act", lambda e, stg=stg, ps=ps, c4=c4: e.activation(out=stg[:, c4 * 512:(c4 + 1) * 512], in_=ps[:, :], func=AF.Copy), r=[pk], w=["stage%d" % s])
            P.op("sp", lambda e, stg=stg, dst=dst: e.dma_start(out=dst, in_=stg), r=["stage%d" % s], dma=True)


def ffn_tile(k, l, T, HT, FT, ACTT, hook_stats=None, hook_h=None):
    P = k.P

    def down(q):
        base = (q % 2) * 11
        for cp in range(8):
            wv, wk = wload(k, ("d", l, q, cp), 11)
            for i in range(2):
                c = cp * 2 + i
                ps, pk = next_ps(k)
                pairs = [(wv[:, j, i * 128:(i + 1) * 128], ACTT[:, base + j, 0:T]) for j in range(11)]
                P.op("pe", mm_group(ps[:, 0:T], pairs), r=[wk] + [("actt", base + j) for j in range(11)], w=[pk])
                if q == 0:
                    P.op("act", lambda e, c=c, ps=ps: e.activation(out=FT[:, c, 0:T], in_=ps[:, 0:T], func=AF.Copy), r=[pk], w=[("ft", c)])
                else:
                    P.op("dve", lambda e, c=c, ps=ps: e.tensor_tensor(FT[:, c, 0:T], FT[:, c, 0:T], ps[:, 0:T], ALU.add), r=[pk, ("ft", c)], w=[("ft", c)])

    qdone = 0
    for p in range(22):
        gv, gk = wload(k, ("g", l, p), 16)
        uv, uk = wload(k, ("u", l, p), 16)
        for i in range(2):
            j = 2 * p + i
            psg, pgk = next_ps(k)
            psu, puk = next_ps(k)
            hkeys = [("ht", kc) for kc in range(16)]
            P.op("pe", mm_group(psg[:, 0:T], [(gv[:, kc, i * 128:(i + 1) * 128], HT[:, kc, 0:T]) for kc in range(16)]),
                 r=[gk] + hkeys, w=[pgk])
            P.op("pe", mm_group(psu[:, 0:T], [(uv[:, kc, i * 128:(i + 1) * 128], HT[:, kc, 0:T]) for kc in range(16)]),
                 r=[uk] + hkeys, w=[puk])
            t, tk = tmp_slot(k, T)
            P.op("act", lambda e, t=t, psg=psg: e.activation(out=t, in_=psg[:, 0:T], func=AF.Silu), r=[pgk], w=[tk])
            slot = ((j // 11) % 2) * 11 + (j % 11)
            P.op("dve", lambda e, t=t, psu=psu, slot=slot: e.tensor_tensor(ACTT[:, slot, 0:T], t, psu[:, 0:T], ALU.mult),
                 r=[tk, puk], w=[("actt", slot)])
        if p == 17 and hook_stats is not None:
            hook_stats()
        if p == 21 and hook_h is not None:
            hook_h()
        while qdone < 4 and 11 * qdone + 10 <= 2 * p + 1:
            down(qdone)
            qdone += 1


def ffn_pass(k, l, src_buf, dst_buf):
    T = 512
    XT = [fview(k, 0, 16, T), fview(k, 8192, 16, T)]
    HT = bview(k, 16384, 16, T)
    ACTT = bview(k, 20480, 22, T)
    FT = fview(k, 26112, 16, T)
    ntile = NTOK // T
    hout = lambda c: HT[:, c, 0:T]
    hkey = lambda c: ("ht", c)
    load_x(k, src_buf, 0, T, XT[0], "xt0")
    for i in range(ntile):
        X, xkey = XT[i % 2], "xt%d" % (i % 2)
        ci = 0 if i * T < SEQ_S else 1
        if i + 1 < ntile:
            load_x(k, src_buf, (i + 1) * T, T, XT[(i + 1) % 2], "xt%d" % ((i + 1) % 2))
        prenorm(k, X, xkey, T, l, 1, ci, hout, hkey)
        ffn_tile(k, l, T, HT, FT, ACTT)
        postnorm_residual(k, FT, "ft", X, xkey, 0, T, l, 1, ci)
        store_x(k, dst_buf, i * T, T, FT, "ft")


def gmlp_pass(k, l, a, src_buf, dst_buf):
    P = k.P
    T = 512
    XT = [fview(k, 0, 16, T), fview(k, 8192, 16, T)]
    HT = bview(k, 16384, 16, T)
    FT = fview(k, 20480, 16, T)
    U = bview(k, 20480, 16, T)
    V = k.arena[:, 24576:28672].bitcast(BF16).rearrange("p (j f) -> p j f", f=2048)
    wstage = fview(k, 29696, 8, 128)
    GT = bview(k, 29696, 16, T)
    smv = k.sm
    P.op("sp", lambda e: e.dma_start(out=wstage, in_=k.I["a_w_s"][a].rearrange("g p q -> p g q")), w=["wstage"], dma=True)
    P.op("sp", lambda e: e.dma_start(out=k.gm_bs[0:1, :], in_=k.I["a_b_s"][a:a + 1].rearrange("a g p -> a (g p)")), w=["bsrow"], dma=True)
    for g in range(8):
        ps, pk = next_ps(k)
        P.op("pe", lambda e, ps=ps, g=g: e.transpose(ps[:, 0:128], wstage[:, g, :], k.ident[:]), r=["wstage"], w=[pk])
        P.op("dve", lambda e, ps=ps, g=g: e.tensor_copy(k.gm_ws[:, g * 128:(g + 1) * 128], ps[:, 0:128]), r=[pk], w=["gm_ws"])
    onesrow = k.arena[0:1, 30720:30784].bitcast(BF16)
    hi = k.arena[0:1, 30848:31360].bitcast(BF16)
    lo = k.arena[0:1, 31360:31872].bitcast(BF16)
    hi32 = k.arena[0:1, 31872:32896]
    P.op("dve", lambda e: e.memset(onesrow, 1.0), w=["onesrow"])
    P.op("dve", lambda e: e.tensor_copy(hi, k.gm_bs[0:1, :]), r=["bsrow"], w=["bshi"])
    P.op("dve", lambda e: e.tensor_copy(hi32, hi), r=["bshi"], w=["bshi32"])
    P.op("dve", lambda e: e.tensor_tensor(hi32, k.gm_bs[0:1, :], hi32, ALU.subtract), r=["bsrow", "bshi32"], w=["bshi32"])
    P.op("dve", lambda e: e.tensor_copy(lo, hi32), r=["bshi32"], w=["bslo"])
    for h in range(2):
        ps, pk = next_ps(k)
        P.op("pe", mm_group(ps[:, 0:512], [(onesrow, hi[:, h * 512:(h + 1) * 512]), (onesrow, lo[:, h * 512:(h + 1) * 512])]),
             r=["onesrow", "bshi", "bslo"], w=[pk])
        P.op("act", lambda e, ps=ps, h=h: e.activation(out=wstage_b(k)[:, h * 512:(h + 1) * 512], in_=ps[:, 0:512], func=AF.Copy), r=[pk], w=["bsb"])
    BSB = wstage_b(k)
    barrier(k)

    ntile = NTOK // T
    hout = lambda c: HT[:, c, 0:T]
    hkey = lambda c: ("ht", c)
    gkeys = [("gt", kc) for kc in range(16)]
    load_x(k, src_buf, 0, T, XT[0], "xt0")
    prenorm(k, XT[0], "xt0", T, l, 0, 0, hout, hkey)
    for i in range(ntile):
        X, xkey = XT[i % 2], "xt%d" % (i % 2)
        if i + 1 < ntile:
            load_x(k, src_buf, (i + 1) * T, T, XT[(i + 1) % 2], "xt%d" % ((i + 1) % 2))
        ci = 0 if i * T < SEQ_S else 1
        hkeys = [("ht", kc) for kc in range(16)]
        P.op("pool", lambda e: e.memset(smv[:, 0:32], 0.0), w=["ssq"])
        for j in range(8):
            wv, wk = wload(k, ("av", a, j), 16)
            for tc in range(4):
                ps, pk = next_ps(k)
                P.op("pe", mm_group(ps[:, 0:256], [(HT[:, kc, tc * 128:(tc + 1) * 128], wv[:, kc, :]) for kc in range(16)]),
                     r=[wk] + hkeys, w=[pk])
                t, tk = tmp_slot(k, 256)
                P.op("act", lambda e, ps=ps, tc=tc, j=j, t=t: e.activation(out=t, in_=ps[:, 0:256], func=AF.Square,
                                                                         accum_out=smv[:, tc * 8 + j:tc * 8 + j + 1]), r=[pk], w=[tk, "ssq"])
                P.op("dve", lambda e, ps=ps, tc=tc, j=j: e.tensor_copy(V[:, tc, j * 256:(j + 1) * 256], ps[:, 0:256]), r=[pk], w=[pk, ("ft", 8 + 2 * tc), ("ft", 9 + 2 * tc)])
        P.op("dve", lambda e: e.tensor_reduce(smv[:, 32:36], smv[:, 0:32].rearrange("p (t j) -> p t j", j=8), mybir.AxisListType.X, ALU.add),
             r=["ssq"], w=["rv"])
        P.op("act", lambda e: e.activation(out=smv[:, 36:40], in_=smv[:, 32:36], func=AF.Sqrt, bias=k.epsb[:, 0:1], scale=1.0 / D), r=["rv"], w=["rv2"])
        P.op("dve", lambda e: e.reciprocal(smv[:, 40:44], smv[:, 36:40]), r=["rv2"], w=["rv3"])
        for tc in range(4):
            P.op("dve", lambda e, tc=tc: e.tensor_scalar(k.gm_wsr[:, tc * 1024:(tc + 1) * 1024], k.gm_ws[:, :], smv[:, 40 + tc:41 + tc], None, ALU.mult),
                 r=["rv3", "gm_ws"], w=[("wsr", tc)])
        for j in range(8):
            wv, wk = wload(k, ("au", a, j), 16)
            for ii in range(2):
                fc = 2 * j + ii
                ps, pk = next_ps(k)
                P.op("pe", mm_group(ps[:, 0:T], [(wv[:, kc, ii * 128:(ii + 1) * 128], HT[:, kc, 0:T]) for kc in range(16)]),
                     r=[wk] + hkeys, w=[pk])
                P.op("act", lambda e, ps=ps, fc=fc: e.activation(out=U[:, fc, 0:T], in_=ps[:, 0:T], func=AF.Copy), r=[pk], w=[("ft", fc // 2)])
        if i + 1 < ntile:
            cin = 0 if (i + 1) * T < SEQ_S else 1
            prenorm(k, XT[(i + 1) % 2], "xt%d" % ((i + 1) % 2), T, l, 0, cin, hout, hkey)
        for fc in range(16):
            g = fc // 2
            ps, pk = next_ps(k)

            def fn(e, ps=ps, fc=fc, g=g):
                ins = None
                for tc in range(4):
                    ins = e.matmul(ps[:, tc * 128:(tc + 1) * 128], V[:, tc, fc * 128:(fc + 1) * 128],
                                   k.gm_wsr[:, tc * 1024 + g * 128:tc * 1024 + (g + 1) * 128], start=True, stop=True)
                return ins
            P.op("pe", fn, r=[("ft", 8 + jj) for jj in range(8)] + [("wsr", tc) for tc in range(4)], w=[pk])
            t, tk = tmp_slot(k, T)
            for tc in range(4):
                P.op("dve", lambda e, ps=ps, t=t, tc=tc, fc=fc, g=g: e.scalar_tensor_tensor(
                    t[:, tc * 128:(tc + 1) * 128], ps[:, tc * 128:(tc + 1) * 128], k.misc[:, a * 16 + fc:a * 16 + fc + 1],
                    BSB[:, g * 128:(g + 1) * 128], ALU.mult, ALU.add), r=[pk, "bsb"], w=[tk])
            P.op("pool", lambda e, t=t, fc=fc: e.tensor_tensor(GT[:, fc, 0:T], t, U[:, fc, 0:T], ALU.mult),
                 r=[tk, ("ft", fc // 2)], w=[("gt", fc)])
        for j in range(8):
            wv, wk = wload(k, ("ao", a, j), 16)
            for ii in range(2):
                c = 2 * j + ii
                ps, pk = next_ps(k)
                P.op("pe", mm_group(ps[:, 0:T], [(wv[:, kc, ii * 128:(ii + 1) * 128], GT[:, kc, 0:T]) for kc in range(16)]),
                     r=[wk] + gkeys, w=[pk])
                P.op("act", lambda e, ps=ps, c=c: e.activation(out=FT[:, c, 0:T], in_=ps[:, 0:T], func=AF.Copy), r=[pk], w=[("ft", c)])
        postnorm_residual(k, FT, "ft", X, xkey, 0, T, l, 0, ci)
        store_x(k, dst_buf, i * T, T, FT, "ft")


def wstage_b(k):
    return k.arena[:, 28672:29696]


def run_layers(k, stop_after, dbg):
    P = k.P
    ingest_pass(k)
    barrier(k)
    sub = 0
    src = 0
    for l in range(DEPTH):
        for s in range(2):
            if sub > stop_after:
                break
            if s == 0:
                kind = l % 3
                if kind == 0:
                    gmlp_pass(k, l, l // 3, src, 1 - src)
                elif kind == 1:
                    attn_pass(k, l, src, 1 - src)
                else:
                    pool_pass(k, l, src, 1 - src)
            else:
                ffn_pass(k, l, src, 1 - src)
            src = 1 - src
            sub += 1
            barrier(k)
    if dbg:
        T = 512
        X = fview(k, 0, 16, T)
        dv = k.O["dbg"].rearrange("p (c t) -> p c t", t=NTOK)
        for i in range(NTOK // T):
            P.op("pool", lambda e, i=i: e.dma_start(out=X[:, :, :], in_=k.xscr[src][:, :, i * T:(i + 1) * T]), w=["dx"], dma=True)
            P.op("pool", lambda e, i=i: e.dma_start(out=dv[:, :, i * T:(i + 1) * T], in_=X[:, :, :]), r=["dx"], w=["dbgout"], dma=True)
        barrier(k)
    emit_pass(k, src)


def pool_pass(k, l, src_buf, dst_buf):
    P = k.P
    W, TO = 272, 256
    XS = [fview(k, 0, 16, W), fview(k, 4352, 16, W)]
    G = [fview(k, 8704 + i * 1088, 4, W) for i in range(4)]
    PTS = [bview(k, 13056, 16, TO), bview(k, 15104, 16, TO)]
    FT = fview(k, 17152, 16, TO)
    tiles = [(0, s_, SEQ_S, 0) for s_ in range(0, SEQ_S, TO)] + [(SEQ_S + 256 * b, 0, 256, 1) for b in range(4)]
    k.nps = 6

    def stage_a(i):
        seq0, s_, L, ci = tiles[i]
        par = i % 2
        X, xk = XS[par], "xp%d" % par
        PT, pk_ = PTS[par], "pp%d" % par
        first, last = (s_ == 0), (s_ + TO == L)
        lo, hi = s_ - 8, s_ + TO + 8
        clo, chi = max(lo, 0), min(hi, L)
        xkeys = [(xk, c) for c in range(NCH)]
        if first:
            P.op("pool", lambda e: e.memset(X[:, :, 0:8], 0.0), w=xkeys)
        if last:
            P.op("pool", lambda e: e.memset(X[:, :, 264:272], 0.0), w=xkeys)
        P.op("pool", lambda e: e.dma_start(out=X[:, :, clo - lo:chi - lo], in_=k.xscr[src_buf][:, :, seq0 + clo:seq0 + chi]), w=xkeys, dma=True)
        yield
        rs, rkey = rms_stats(k, lambda c: X[:, c, 0:W], lambda c: (xk, c), NCH, W, 0, k.onesD, bank=7)
        yield
        for g in range(4):
            w = 2 << g
            H, hk = (G[0], "G0") if g % 2 == 0 else (G[3], "G3")
            A, B_ = G[1], G[2]
            ve = "dve" if g % 2 == 0 else "pool"
            for cc in range(4):
                c = 4 * g + cc
                t, tk = tmp_slot(k, W)
                P.op("dve", lambda e, c=c, t=t: e.tensor_tensor(t, X[:, c, 0:W], rs, ALU.mult), r=[(xk, c), rkey], w=[tk])
                P.op("act", lambda e, c=c, cc=cc, t=t, H=H: e.activation(out=H[:, cc, 0:W], in_=t, func=AF.Identity,
                                                                       bias=k.ABv[:, l, 0, ci, 1, c:c + 1], scale=k.ABv[:, l, 0, ci, 0, c:c + 1]),
                     r=[tk], w=[hk])
            yield
            if first:
                P.op(ve, lambda e, H=H: e.memset(H[:, :, 0:8], 0.0), w=[hk])
            if last:
                P.op(ve, lambda e, H=H: e.memset(H[:, :, 264:272], 0.0), w=[hk])
            P.op(ve, lambda e, H=H: e.tensor_tensor(A[:, :, 1:272], H[:, :, 0:271], H[:, :, 1:272], ALU.add), r=[hk], w=["G1"])
            S, sk_ = A, "G1"
            if w >= 4:
                P.op(ve, lambda e: e.tensor_tensor(B_[:, :, 2:271], A[:, :, 1:270], A[:, :, 3:272], ALU.add), r=["G1"], w=["G2"])
                S, sk_ = B_, "G2"
            if w >= 8:
                P.op(ve, lambda e: e.tensor_tensor(A[:, :, 4:268], B_[:, :, 2:266], B_[:, :, 6:270], ALU.add), r=["G2"], w=["G1"])
                S, sk_ = A, "G1"
            if w >= 16:
                P.op(ve, lambda e: e.tensor_tensor(B_[:, :, 8:264], A[:, :, 4:260], A[:, :, 12:268], ALU.add), r=["G1"], w=["G2"])
                S, sk_ = B_, "G2"
            pkeys = [(pk_, 4 * g + cc) for cc in range(4)]
            P.op("dve", lambda e, S=S, H=H, g=g, w=w: e.scalar_tensor_tensor(PT[:, 4 * g:4 * g + 4, 0:TO], S[:, :, 8:264], 1.0 / w, H[:, :, 8:264],
                                                                            ALU.mult, ALU.subtract), r=[sk_, hk], w=pkeys)
            for (flag, c0, p0, io) in ((first, 8, 0, 0), (last, 256, 248, 8)):
                if not flag:
                    continue
                for cc in range(4):
                    t, tk = tmp_slot(k, 8)
                    P.op(ve, lambda e, S=S, cc=cc, t=t, c0=c0, io=io, g=g: e.tensor_tensor(t, S[:, cc, c0:c0 + 8], k.pinv[:, g * 16 + io:g * 16 + io + 8], ALU.mult),
                         r=[sk_], w=[tk])
                    P.op(ve, lambda e, H=H, cc=cc, t=t, c0=c0, p0=p0, g=g: e.tensor_tensor(PT[:, 4 * g + cc, p0:p0 + 8], t, H[:, cc, c0:c0 + 8], ALU.subtract),
                         r=[tk, hk], w=[(pk_, 4 * g + cc)])
            yield

    def stage_b(i):
        seq0, s_, L, ci = tiles[i]
        par = i % 2
        X, xk = XS[par], "xp%d" % par
        PT, pk_ = PTS[par], "pp%d" % par
        for g in range(4):
            pkeys = [(pk_, 4 * g + cc) for cc in range(4)]
            for j in range(2):
                wv, wk = wload(k, ("cp", g, j), 4)
                for ii in range(2):
                    c = 4 * g + 2 * j + ii
                    ps, pk = next_ps(k)
                    P.op("pe", mm_group(ps[:, 0:TO], [(wv[:, kc, ii * 128:(ii + 1) * 128], PT[:, 4 * g + kc, 0:TO]) for kc in range(4)]),
                         r=[wk] + pkeys, w=[pk])
                    P.op("act", lambda e, ps=ps, c=c: e.activation(out=FT[:, c, 0:TO], in_=ps[:, 0:TO], func=AF.Copy, scale=k.misc[:, 32 + c:33 + c]),
                         r=[pk], w=[("ft", c)])
            yield
        rs, rkey = rms_stats(k, lambda c: FT[:, c, 0:TO], lambda c: ("ft", c), NCH, TO, 1, k.onesD, bank=6)
        yield
        for half in range(2):
            for c in range(8 * half, 8 * half + 8):
                t, tk = tmp_slot(k, TO)
                P.op("dve", lambda e, c=c, t=t: e.scalar_tensor_tensor(t, FT[:, c, 0:TO], k.ABv[:, l, 0, ci, 2, c:c + 1], rs, ALU.mult, ALU.mult),
                     r=[("ft", c), rkey], w=[tk])
                P.op("pool", lambda e, c=c, t=t, X=X: e.tensor_tensor(FT[:, c, 0:TO], t, X[:, c, 8:8 + TO], ALU.add),
                     r=[tk, (xk, c)], w=[("ft", c)])
            yield
        store_x(k, dst_buf, seq0 + s_, TO, FT, "ft")

    def run(gens):
        gens = list(gens)
        while gens:
            for g_ in list(gens):
                try:
                    next(g_)
                except StopIteration:
                    gens.remove(g_)

    n = len(tiles)
    run([stage_a(0)])
    for i in range(n):
        run([stage_b(i)] + ([stage_a(i + 1)] if i + 1 < n else []))
    k.nps = 7


def attn_pass(k, l, src_buf, dst_buf):
    P = k.P
    T = 256
    X = fview(k, 0, 16, T)
    HT = bview(k, 4096, 16, T)
    QTf = k.arena[:, 6144:8192].bitcast(BF16)
    OT = bview(k, 8192, 16, T)
    MT = fview(k, 10240, 16, T)
    KT = bview(k, 14336, 4, 4608)
    VA = bview(k, 23552, 36, 512)
    CS = k.arena[:, 32768:33792]
    SKs = k.arena[:, 32768:33792].rearrange("p (t n) -> p t n", n=512)
    SVs = k.arena[:, 12288:13312].rearrange("p (t n) -> p t n", n=512)
    cstage = fview(k, 10240, 4, 512)
    PTs = [k.gm_wsr[:, i * 512:(i + 1) * 512] for i in range(8)]
    sqv = k.sq[:].rearrange("p (s t) -> p s t", t=512)
    st = {"pt": 0}
    scale = 128.0 ** -0.5
    hkeys = [("ht", kc) for kc in range(16)]

    def qk_gen(ps, pk, gcol, rope, out_bf, okeys, box=None, rbank=None, own_stats=False):
        s = k.sq_i % 4
        k.sq_i += 1
        P.op("act", lambda e: e.activation(out=sqv[:, s, :], in_=ps, func=AF.Square), r=[pk], w=["sq%d" % s])
        if own_stats:
            sb_, sbk = next_ps(k)
        else:
            sb_, sbk = k.psum[7], "ps7"
        P.op("pe", lambda e: e.matmul(sb_, k.onesH[:], sqv[:, s, :], start=True, stop=True), r=["sq%d" % s], w=[sbk])
        yield
        rs, rsk = tmp_slot(k, 512)
        P.op("act", lambda e: e.activation(out=rs, in_=sb_, func=AF.Ln, bias=k.epsb[:, 0:1], scale=1.0), r=[sbk], w=[rsk])
        yield
        P.op("act", lambda e: e.activation(out=rs, in_=rs, func=AF.Exp, scale=-0.5), r=[rsk], w=[rsk])
        yield
        qn, qk_ = tmp_slot(k, 512)
        P.op("dve", lambda e: e.scalar_tensor_tensor(qn, ps, k.misc[:, gcol:gcol + 1], rs, ALU.mult, ALU.mult), r=[pk, rsk], w=[qk_])
        if box is not None:
            box["qn"] = (qn, qk_)
        yield
        ov = out_bf
        if not rope:
            P.op("act", lambda e: e.activation(out=ov, in_=qn.rearrange("p (h t) -> p h t", t=256), func=AF.Copy), r=[qk_], w=okeys)
            return
        s2 = k.sq_i % 4
        k.sq_i += 1
        P.op("act", lambda e: e.activation(out=sqv[:, s2, :], in_=qn, func=AF.Copy), r=[qk_], w=["sq%d" % s2])
        if rbank is None:
            pr, prk = next_ps(k)
        else:
            pr, prk = k.psum[rbank], "ps%d" % rbank
        P.op("pe", lambda e: e.matmul(pr, k.rmat[:], sqv[:, s2, :], start=True, stop=True), r=["sq%d" % s2], w=[prk])
        yield
        t1, t1k = tmp_slot(k, 512)
        t2, t2k = tmp_slot(k, 512)
        P.op("dve", lambda e: e.tensor_tensor(t1, qn, CS[:, 0:512], ALU.mult), r=[qk_, "cs"], w=[t1k])
        P.op("dve", lambda e: e.tensor_tensor(t2, pr, CS[:, 512:1024], ALU.mult), r=[prk, "cs"], w=[t2k])
        yield
        P.op("pool", lambda e: e.tensor_tensor(ov, t1.rearrange("p (h t) -> p h t", t=256), t2.rearrange("p (h t) -> p h t", t=256), ALU.add),
             r=[t1k, t2k], w=okeys)

    def run_gens(gens):
        gens = [g for g in gens if g is not None]
        while gens:
            for g in list(gens):
                try:
                    next(g)
                except StopIteration:
                    gens.remove(g)

    def proj_pair(unit, which, bank=None, pre=None):
        wv, wk = pre if pre is not None else wload(k, ("qkv", unit), 16)
        if bank is None:
            ps, pk = next_ps(k)
        else:
            ps, pk = k.psum[bank], "ps%d" % bank

        def fn(e):
            ins = None
            for hh in range(2):
                for kc in range(16):
                    ins = e.matmul(ps[:, hh * 256:(hh + 1) * 256], wv[:, kc, hh * 128:(hh + 1) * 128], HT[:, kc, 0:T],
                                   start=(kc == 0), stop=(kc == 15))
            return ins
        P.op("pe", fn, r=[wk] + hkeys, w=[pk])
        return ps, pk

    def v_part(chunk0, state_b=None):
        for j in range(2):
            wv, wk = wload(k, ("qkv", 10 + j), 16)
            for tc in range(2):
                ps, pk = next_ps(k)
                P.op("pe", mm_group(ps[:, 0:256], [(HT[:, kc, tc * 128:(tc + 1) * 128], wv[:, kc, :]) for kc in range(16)]),
                     r=[wk] + hkeys, w=[pk])
                P.op("dve", lambda e, ps=ps, tc=tc, j=j: e.tensor_copy(VA[:, chunk0 + tc, j * 256:(j + 1) * 256], ps[:, 0:256]),
                     r=[pk], w=[("va", chunk0 + tc)])
                if state_b is not None:
                    P.op("dve", lambda e, ps=ps, tc=tc, j=j: e.tensor_copy(SVs[:, tc, j * 256:(j + 1) * 256], ps[:, 0:256]),
                         r=[pk], w=["svs"] + [("mt", c) for c in range(8, 12)])
        if state_b is not None:
            b = state_b
            P.op("pool", lambda e: e.dma_start(out=k.O["sv"][b * 256:(b + 1) * 256, :].rearrange("(t p) n -> p t n", p=128), in_=SVs),
                 r=["svs"], dma=True)

    def k_part(key0, rope, state_b=None):
        gens, boxes = [], []
        for jp in range(2):
            ps, pk = proj_pair(8 + jp, "k")
            box = {}
            boxes.append(box)
            gens.append(qk_gen(ps, pk, 81, rope, KT[:, 2 * jp:2 * jp + 2, key0:key0 + T], [("kt", 2 * jp), ("kt", 2 * jp + 1)], box=box,
                               own_stats=(jp == 1)))
        run_gens(gens)
        if state_b is None:
            return
        for jp in range(2):
            qn, qk_ = boxes[jp]["qn"]
            pt_, ptk = next_ps(k)

            def fn(e, qn=qn, pt_=pt_):
                ins = None
                for tc in range(2):
                    for hh in range(2):
                        ins = e.transpose(pt_[:, (tc * 2 + hh) * 128:(tc * 2 + hh + 1) * 128],
                                          qn[:, hh * 256 + tc * 128:hh * 256 + (tc + 1) * 128], k.ident[:])
                return ins
            P.op("pe", fn, r=[qk_], w=[ptk])
            P.op("act", lambda e, pt_=pt_, jp=jp: e.activation(out=SKs[:, :, jp * 256:(jp + 1) * 256],
                                                             in_=pt_.rearrange("p (t n) -> p t n", n=256), func=AF.Copy),
                 r=[ptk], w=["sks"])
        b = state_b
        P.op("pool", lambda e: e.dma_start(out=k.O["sk"][b * 256:(b + 1) * 256, :].rearrange("(t p) n -> p t n", p=128), in_=SKs),
             r=["sks"], dma=True)

    def q_part(rope, nkc, ci, X, xkey, tok0):
        def qout(jp):
            return QTf[:, jp * 512:(jp + 1) * 512].rearrange("p (h t) -> p h t", t=256)
        g0 = []
        for jp in range(2):
            ps, pk = proj_pair(jp, "q")
            g0.append(qk_gen(ps, pk, 80, rope, qout(jp), [("qt", jp)], own_stats=(jp == 1)))
        run_gens(g0)
        npair = nkc // 2
        pending = None
        ACC = k.gm_ws
        hl = k.gm_bs[:, 0:512].bitcast(BF16)
        acc1 = k.gm_bs[:, 512:1024]
        for jp in range(8):
            g = jp // 2
            psO, pOk = k.psum[4 + jp % 2], "ps%d" % (4 + jp % 2)
            psD, pDk = k.psum[7], "ps7"
            qv = QTf[:, jp * 512:(jp + 1) * 512]
            gen = None
            if jp == 0:
                prew = {2: wload(k, ("qkv", 2), 16)}
            if jp + 3 < 8:
                prew[jp + 3] = wload(k, ("qkv", jp + 3), 16)
            if jp + 2 < 8:
                ps_n, pk_n = proj_pair(jp + 2, "q", bank=6, pre=prew[jp + 2])
                gen = qk_gen(ps_n, pk_n, 80, rope, qout(jp + 2), [("qt", jp + 2)], rbank=7)

            def s_mm(p, g=g, qv=qv, jp=jp):
                b = 2 * (p % 2)

                def fn(e):
                    ins = None
                    for h in range(2):
                        kc = 2 * p + h
                        ins = e.matmul(k.psum[b + h], KT[:, g, kc * 128:(kc + 1) * 128], qv, start=True, stop=True)
                    return ins
                P.op("pe", fn, r=[("kt", g), ("qt", jp)], w=["ps%d" % b, "ps%d" % (b + 1)])
            s_mm(0)
            if pending is not None:
                pending()
                pending = None
            for p in range(npair):
                if p + 1 < npair:
                    s_mm(p + 1)
                b = 2 * (p % 2)
                pi = st["pt"] % 4
                st["pt"] += 1
                pt = k.gm_wsr[:, pi * 1024:(pi + 1) * 1024]
                for h in range(2):
                    P.op("act", lambda e, b=b, pt=pt, h=h: e.activation(out=pt[:, h * 512:(h + 1) * 512], in_=k.psum[b + h], func=AF.Exp, scale=scale),
                         r=["ps%d" % (b + h)], w=[("pt", pi)])

                def fo(e, p=p, pt=pt, psO=psO, g=g):
                    ins = None
                    for h in range(2):
                        kc = 2 * p + h
                        ins = e.matmul(psO, VA[:, kc, g * 128:(g + 1) * 128], pt[:, h * 512:(h + 1) * 512],
                                       start=(kc == 0), stop=(kc == nkc - 1))
                    return ins
                P.op("pe", fo, r=[("va", 2 * p), ("va", 2 * p + 1), ("pt", pi)], w=[pOk])
                if p == 0:
                    P.op("dve", lambda e, pt=pt: e.tensor_copy(ACC[:, :], pt), r=[("pt", pi)], w=["acc"])
                else:
                    P.op("dve", lambda e, pt=pt: e.tensor_tensor(ACC[:, :], ACC[:, :], pt, ALU.add), r=[("pt", pi), "acc"], w=["acc"])
                if gen is not None and p >= 1:
                    try:
                        next(gen)
                    except StopIteration:
                        gen = None
            run_gens([gen])
            P.op("dve", lambda e: e.tensor_tensor(acc1, ACC[:, 0:512], ACC[:, 512:1024], ALU.add), r=["acc"], w=["acc1"])
            P.op("dve", lambda e: e.tensor_copy(hl[:, 0:512], acc1), r=["acc1"], w=["hl"])
            P.op("dve", lambda e: e.tensor_tensor(hl[:, 512:1024], acc1, hl[:, 0:512], ALU.subtract), r=["acc1", "hl"], w=["hl"])

            def tail(jp=jp, psO=psO, pOk=pOk, psD=psD, pDk=pDk):
                P.op("pe", mm_group(psD, [(k.ones1[:], hl[:, 0:512]), (k.ones1[:], hl[:, 512:1024])]), r=["hl"], w=[pDk])
                rd, rdk = tmp_slot(k, 512)
                P.op("act", lambda e, rd=rd, psD=psD: e.activation(out=rd, in_=psD, func=AF.Ln), r=[pDk], w=[rdk])
                P.op("act", lambda e, rd=rd: e.activation(out=rd, in_=rd, func=AF.Exp, scale=-1.0), r=[rdk], w=[rdk])
                P.op("dve", lambda e, rd=rd, jp=jp, psO=psO: e.tensor_tensor(OT[:, 2 * jp:2 * jp + 2, :], psO.rearrange("p (h t) -> p h t", t=256),
                                                                           rd.rearrange("p (h t) -> p h t", t=256), ALU.mult),
                     r=[pOk, rdk], w=[("ot", 2 * jp), ("ot", 2 * jp + 1)])
            pending = tail
        if pending is not None:
            pending()
            pending = None
        okeys = [("ot", c) for c in range(16)]
        for j in range(8):
            wv, wk = wload(k, ("bo", j), 16)
            for ii in range(2):
                c = 2 * j + ii
                ps, pk = next_ps(k)
                P.op("pe", mm_group(ps[:, 0:T], [(wv[:, kc, ii * 128:(ii + 1) * 128], OT[:, kc, 0:T]) for kc in range(16)]),
                     r=[wk] + okeys, w=[pk])
                P.op("act", lambda e, ps=ps, c=c: e.activation(out=MT[:, c, 0:T], in_=ps[:, 0:T], func=AF.Copy), r=[pk],
                     w=[("mt", c), "svs", "cstage"])
        postnorm_residual(k, MT, "mt", X, xkey, 0, T, l, 0, ci)
        store_x(k, dst_buf, tok0, T, MT, "mt")

    xkeys = [("xt0", c) for c in range(NCH)]
    k.lnexp = True
    for b in range(4):
        tok0 = SEQ_S + 256 * b
        load_x(k, src_buf, tok0, T, X, "xt0")
        prenorm(k, X, "xt0", T, l, 0, 1, lambda c: HT[:, c, 0:T], lambda c: ("ht", c))
        k_part(0, False, state_b=b)
        v_part(0, state_b=b)
        q_part(False, 2, 1, X, "xt0", tok0)
    P.op("sp", lambda e: e.dma_start(out=cstage, in_=k.I["ck"].rearrange("(kc p) n -> p kc n", p=128)), w=["cstage"] + [("mt", c) for c in range(8)], dma=True)
    for h in range(4):
        ps, pk = next_ps(k)

        def fn(e, ps=ps, h=h):
            ins = None
            for kc in range(4):
                ins = e.transpose(ps[:, kc * 128:(kc + 1) * 128], cstage[:, kc, h * 128:(h + 1) * 128], k.ident[:])
            return ins
        P.op("pe", fn, r=["cstage"], w=[pk])
        P.op("dve", lambda e, ps=ps, h=h: e.tensor_copy(KT[:, h, 4096:4608], ps[:, :]), r=[pk], w=[("kt", h)])
    P.op("sp", lambda e: e.dma_start(out=cstage, in_=k.I["cv"].rearrange("(kc p) n -> p kc n", p=128)), w=["cstage"], dma=True)
    P.op("dve", lambda e: e.tensor_copy(VA[:, 32:36, :], cstage), r=["cstage"], w=[("va", 32 + i) for i in range(4)])
    for i in range(16):
        load_x(k, src_buf, i * T, T, X, "xt0")
        P.op("sp", lambda e, i=i: e.dma_start(out=CS, in_=k.I["cs_tab"][i]), w=["cs", "sks"], dma=True)
        prenorm(k, X, "xt0", T, l, 0, 0, lambda c: HT[:, c, 0:T], lambda c: ("ht", c))
        k_part(i * T, True)
        v_part(2 * i)
    for i in range(16):
        load_x(k, src_buf, i * T, T, X, "xt0")
        P.op("sp", lambda e, i=i: e.dma_start(out=CS, in_=k.I["cs_tab"][i]), w=["cs"], dma=True)
        prenorm(k, X, "xt0", T, l, 0, 0, lambda c: HT[:, c, 0:T], lambda c: ("ht", c))
        q_part(True, 36, 0, X, "xt0", i * T)
    k.lnexp = False


def host_consts():
    ident = np.eye(128, dtype=np.float32)
    rmat = np.zeros((128, 128), np.float32)
    for j in range(64):
        rmat[2 * j + 1, 2 * j] = -1.0
        rmat[2 * j, 2 * j + 1] = 1.0
    t = np.arange(SEQ_S)
    row = (t // 64).astype(np.float32)
    col = (t % 64).astype(np.float32)
    inv = np.power(np.float32(10000.0), -np.arange(0, 64, 2, dtype=np.float32) / np.float32(64)).astype(np.float32)
    ang = np.concatenate([row[:, None] * inv, col[:, None] * inv], axis=-1).astype(np.float32)
    cos = np.repeat(np.cos(ang), 2, axis=1).T.astype(np.float32)
    sin = np.repeat(np.sin(ang), 2, axis=1).T.astype(np.float32)
    cs = np.zeros((16, 128, 2, 2, 256), np.float32)
    for i in range(16):
        for d in range(2):
            cs[i, :, 0, d, :] = cos[:, i * 256:(i + 1) * 256]
            cs[i, :, 1, d, :] = sin[:, i * 256:(i + 1) * 256]
    cs = cs.reshape(16, 128, 1024)
    pinv = np.zeros((128, 64), np.float32)
    for wi, w in enumerate((2, 4, 8, 16)):
        for i in range(8):
            cnt_lo = (i + w // 2) - max(i - w // 2, 0)
            cnt_hi = (8 - i + w // 2) if (i + w // 2 > 8) else w
            pinv[:, wi * 16 + i] = 1.0 / cnt_lo
            pinv[:, wi * 16 + 8 + i] = 1.0 / cnt_hi
    return {"ident": ident, "rmat": rmat, "cs_tab": cs, "pool_inv": pinv}


_WNAMES = ["w_mod", "b_mod", "norm_mix_pre", "norm_mix_post", "norm_ffn_pre", "norm_ffn_post",
           "a_w_in", "a_norm_v", "a_w_s", "a_b_s", "a_w_out", "b_w_qkv", "b_q_norm", "b_k_norm", "b_w_o",
           "c_scale", "f_w_gu", "f_w_down"]


def make_in_maps(inp, ncores=8):
    consts = host_consts()
    shared = {n: np.ascontiguousarray(np.asarray(inp[n], dtype=np.float32)) for n in _WNAMES}
    shared["c_w_pool"] = np.ascontiguousarray(np.asarray(inp["c_w_pool"], dtype=np.float32).reshape(4, 512, 512))
    shared.update(consts)
    maps = []
    for i in range(ncores):
        m = dict(shared)
        m["xs"] = np.ascontiguousarray(inp["x_sample"][i])
        m["xp"] = np.ascontiguousarray(np.asarray(inp["x_prompt"][4 * i:4 * i + 4]).reshape(1024, D))
        m["ck"] = np.ascontiguousarray(np.asarray(inp["cache_k"][i, 0]).reshape(512, 512))
        m["cv"] = np.ascontiguousarray(np.asarray(inp["cache_v"][i, 0]).reshape(512, 512))
        m["cond2"] = np.ascontiguousarray(np.stack([np.asarray(inp["c"][i]), np.asarray(inp["c_ctx"])]).astype(np.float32))
        maps.append(m)
    return maps


def kernel(**inp):
    nc = build()
    maps = make_in_maps(inp)
    res = run_bass_kernel_spmd(nc, maps, core_ids=list(range(8)))
    r = res.results
    yp = np.concatenate([r[i]["yp"].reshape(4, 256, D) for i in range(8)], axis=0)
    ys = np.stack([r[i]["ys"] for i in range(8)], axis=0)
    sk = np.concatenate([r[i]["sk"].reshape(4, 1, 256, 4, 128) for i in range(8)], axis=0)
    sv = np.concatenate([r[i]["sv"].reshape(4, 1, 256, 4, 128) for i in range(8)], axis=0)
    return (yp.astype(np.float32), ys.astype(np.float32), sk.astype(np.float32), sv.astype(np.float32))
```

```python
from contextlib import ExitStack
import numpy as np
import concourse.bass as bass
import concourse.mybir as mybir
from concourse.bass_utils import run_bass_kernel_spmd

F32 = mybir.dt.float32
BF16 = mybir.dt.bfloat16
AF = mybir.ActivationFunctionType
ALU = mybir.AluOpType

D = 2048
NCH = 16
DEPTH = 4
SEQ_S = 4096
NTOK = 5120
HID = 5632
NHC = 44
EPS = 1e-6
NDMA = 8
NWSLOT = 3
ENGS = ["pe", "act", "dve", "pool", "sp"]


class Prog:
    def __init__(self, nc, es):
        self.nc = nc
        self.sem = {}
        names = ["pe", "act", "dve", "pool"]
        for q_ in ("sp", "pool", "act", "bg"):
            names += ["d%s%d" % (q_, i) for i in range(NDMA)]
        for n in names:
            self.sem[n] = es.enter_context(nc.semaphore("s_" + n))
        self.dma_i = {"sp": 0, "pool": 0, "act": 0, "bg": 0}
        self.cnt = {n: 0 for n in self.sem}
        self.known = {e: {} for e in ENGS}
        self.lastw = {}
        self.reads = {}
        self.q = {e: [] for e in ENGS}
        self.nops = 0

    def op(self, eng, fn, r=(), w=(), dma=False):
        waits = {}

        def need(t):
            if t is None:
                return
            s, v = t
            if v > 0 and waits.get(s, 0) < v:
                waits[s] = v

        for k in r:
            need(self.lastw.get(k))
        for k in w:
            need(self.lastw.get(k))
            for s, v in self.reads.get(k, {}).items():
                need((s, v))
        if dma:
            qn = "bg" if dma == "bg" else eng
            s = "d%s%d" % (qn, self.dma_i[qn] % NDMA)
            self.dma_i[qn] += 1
            need((s, self.cnt[s]))
            self.cnt[s] += 16
            amt = 16
        else:
            s = eng
            self.cnt[s] += 1
            amt = 1
        ticket = (s, self.cnt[s])
        wl = []
        kn = self.known[eng]
        for s2, v in waits.items():
            if kn.get(s2, 0) < v:
                wl.append((s2, v))
                kn[s2] = v
        self.q[eng].append((fn, wl, s, amt))
        for k in r:
            self.reads.setdefault(k, {})[ticket[0]] = ticket[1]
        for k in w:
            self.lastw[k] = ticket
            self.reads[k] = {}
        self.nops += 1
        return ticket

    def flush(self, final=False):
        nc = self.nc
        q = self.q
        sem = self.sem
        cnt = self.cnt

        def run(e, name):
            for fn, wl, s, amt in q[name]:
                for s2, v in wl:
                    e.wait_ge(sem[s2], v)
                ins = fn(e)
                ins.then_inc(sem[s], amt)
            for s2, v in cnt.items():
                if v > 0:
                    e.wait_ge(sem[s2], v)

        with nc.Block() as blk:
            @blk.tensor
            def _(e):
                run(e, "pe")

            @blk.scalar
            def _(e):
                run(e, "act")

            @blk.vector
            def _(e):
                run(e, "dve")

            @blk.gpsimd
            def _(e):
                run(e, "pool")

            @blk.sync
            def _(e):
                run(e, "sp")
        self.q = {e: [] for e in ENGS}
        for e in ENGS:
            self.known[e] = dict(cnt)


def mm_group(out, pairs, start=True):
    def fn(pe):
        n = len(pairs)
        ins = None
        for i, (a, b) in enumerate(pairs):
            ins = pe.matmul(out, a, b, start=(start and i == 0), stop=(i == n - 1))
        return ins
    return fn


class K:
    pass


def weight_units():
    units = []
    idx = {}

    def add(key, name, l, k0, nK, col0):
        idx[key] = len(units)
        units.append((name, l, k0, nK, col0))

    def ffn(l):
        for j in range(NHC // 2):
            add(("g", l, j), "f_w_gu", l, 0, 16, j * 256)
            add(("u", l, j), "f_w_gu", l, 0, 16, HID + j * 256)
        for q in range(4):
            for cp in range(8):
                add(("d", l, q, cp), "f_w_down", l, q * 11, 11, cp * 256)

    def gm(a):
        for j in range(8):
            add(("au", a, j), "a_w_in", a, 0, 16, j * 256)
        for j in range(8):
            add(("av", a, j), "a_w_in", a, 0, 16, 2048 + j * 256)
        for j in range(8):
            add(("ao", a, j), "a_w_out", a, 0, 16, j * 256)

    gm(0)
    ffn(0)
    for j in range(12):
        add(("qkv", j), "b_w_qkv", 0, 0, 16, j * 256)
    for j in range(8):
        add(("bo", j), "b_w_o", 0, 0, 16, j * 256)
    ffn(1)
    for g in range(4):
        for j in range(2):
            add(("cp", g, j), "c_w_pool", g, 0, 4, j * 256)
    ffn(2)
    gm(1)
    ffn(3)
    return units, idx


def build(stop_after=99, dbg=False):
    nc = bass.Bass("TRN2", target_bir_lowering=False)
    es = ExitStack()
    with es:
        return _build(nc, es, stop_after, dbg)


def _build(nc, es, stop_after, dbg):
    def din(name, shape):
        return nc.dram_tensor(name, list(shape), F32, kind="ExternalInput").ap()

    def dout(name, shape):
        return nc.dram_tensor(name, list(shape), F32, kind="ExternalOutput").ap()

    I = {}
    I["xs"] = din("xs", [SEQ_S, D])
    I["xp"] = din("xp", [1024, D])
    I["ck"] = din("ck", [512, 512])
    I["cv"] = din("cv", [512, 512])
    I["cond2"] = din("cond2", [2, D])
    I["w_mod"] = din("w_mod", [DEPTH, D, 6 * D])
    I["b_mod"] = din("b_mod", [DEPTH, 6 * D])
    for n in ["norm_mix_pre", "norm_mix_post", "norm_ffn_pre", "norm_ffn_post"]:
        I[n] = din(n, [DEPTH, D])
    I["a_w_in"] = din("a_w_in", [2, D, 2 * D])
    I["a_norm_v"] = din("a_norm_v", [2, D])
    I["a_w_s"] = din("a_w_s", [2, 8, 128, 128])
    I["a_b_s"] = din("a_b_s", [2, 8, 128])
    I["a_w_out"] = din("a_w_out", [2, D, D])
    I["b_w_qkv"] = din("b_w_qkv", [1, D, 3072])
    I["b_q_norm"] = din("b_q_norm", [1, 128])
    I["b_k_norm"] = din("b_k_norm", [1, 128])
    I["b_w_o"] = din("b_w_o", [1, D, D])
    I["c_w_pool"] = din("c_w_pool", [4, 512, 512])
    I["c_scale"] = din("c_scale", [1, D])
    I["f_w_gu"] = din("f_w_gu", [DEPTH, D, 2 * HID])
    I["f_w_down"] = din("f_w_down", [DEPTH, HID, D])
    I["ident"] = din("ident", [128, 128])
    I["rmat"] = din("rmat", [128, 128])
    I["cs_tab"] = din("cs_tab", [16, 128, 1024])
    I["pool_inv"] = din("pool_inv", [128, 64])

    O = {}
    O["yp"] = dout("yp", [1024, D])
    O["ys"] = dout("ys", [SEQ_S, D])
    O["sk"] = dout("sk", [1024, 512])
    O["sv"] = dout("sv", [1024, 512])
    if dbg:
        O["dbg"] = dout("dbg", [128, NCH * NTOK])

    units, uidx = weight_units()
    NU = len(units)
    wb_parts = [nc.dram_tensor("wb%d" % i, [128, 128, 4096], BF16).ap() for i in range((NU + 127) // 128)]

    class _WB:
        def __getitem__(self, key):
            u = key[0]
            return wb_parts[u // 128][(u % 128,) + tuple(key[1:])]
    wb = _WB()
    xscr = [nc.dram_tensor("xscr%d" % i, [128, NCH * NTOK], F32).ap().rearrange("p (c t) -> p c t", t=NTOK)
            for i in range(2)]

    P = Prog(nc, es)
    k = K()
    k.nc, k.P, k.I, k.O, k.wb, k.xscr, k.uidx, k.units = nc, P, I, O, wb, xscr, uidx, units

    def sb(name, shape, dt=F32):
        return es.enter_context(nc.sbuf_tensor("sb_" + name, list(shape), dt))

    k.ident = sb("ident", [128, 128])
    k.identb = sb("identb", [128, 128], BF16)
    k.rmat = sb("rmatb", [128, 128], BF16)
    k.onesD = sb("onesD", [128, 128], BF16)
    k.onesH = sb("onesH", [128, 128], BF16)
    k.ones1 = sb("ones1", [128, 128], BF16)
    k.epsb = sb("epsb", [128, 1])
    k.modT = sb("modT", [128, DEPTH * 6 * NCH * 2])
    k.gT = sb("gT", [128, 4 * DEPTH * NCH])
    k.misc = sb("miscT", [128, 128])
    k.AB = sb("AB", [128, DEPTH * 2 * 2 * 3 * NCH])
    k.pinv = sb("pinv", [128, 64])
    k.bmT = sb("bmT", [128, 384])
    k.scT = sb("scT", [128, 32], BF16)

    with ExitStack() as es0:
        def sb0(name, shape, dt=F32):
            return es0.enter_context(nc.sbuf_tensor("s0_" + name, list(shape), dt))

        def ps0(name):
            return es0.enter_context(nc.psum_tensor(name, [128, 512], F32))

        stgA = sb0("stgA", [128, 128])
        stgB = sb0("stgB", [128, 128])
        stgC = sb0("stgC", [128, 128])
        stgM = [sb0("stgM%d" % i, [128, 128]) for i in range(3)]
        bmT = k.bmT
        rm32 = sb0("rm32", [128, 128])
        scT = k.scT
        wm = [sb0("wm%d" % i, [128, 16 * 768], BF16) for i in range(2)]
        pst = [ps0("pst%d" % i) for i in range(2)]
        psm = [ps0("psm%d" % i) for i in range(4)]

        P.op("sp", lambda e: e.dma_start(out=k.ident[:], in_=I["ident"][:, :]), w=["ident"], dma=True)
        P.op("sp", lambda e: e.dma_start(out=rm32[:], in_=I["rmat"][:, :]), w=["rm32"], dma=True)
        P.op("sp", lambda e: e.dma_start(out=k.pinv[:], in_=I["pool_inv"][:, :]), w=["pinv"], dma=True)
        P.op("dve", lambda e: e.memset(k.onesD[:], 1.0 / D), w=["onesD"])
        P.op("dve", lambda e: e.memset(k.onesH[:], 1.0 / 128), w=["onesH"])
        P.op("dve", lambda e: e.memset(k.ones1[:], 1.0), w=["ones1"])
        P.op("dve", lambda e: e.memset(k.epsb[:], EPS), w=["epsb"])
        P.op("dve", lambda e: e.tensor_copy(k.identb[:], k.ident[:]), r=["ident"], w=["identb"])
        P.op("dve", lambda e: e.tensor_copy(k.rmat[:], rm32[:]), r=["rm32"], w=["rmat"])
        for st_, nm_ in ((stgA, "stgA"), (stgB, "stgB"), (stgC, "stgC")):
            P.op("pool", lambda e, st_=st_: e.memset(st_[:], 0.0), w=[nm_])

        def rows(name):
            return I[name].rearrange("l (c f) -> (l c) f", f=128)

        P.op("sp", lambda e: e.dma_start(out=stgA[0:64, :], in_=rows("norm_mix_pre")), w=["stgA"], dma=True)
        P.op("sp", lambda e: e.dma_start(out=stgA[64:128, :], in_=rows("norm_mix_post")), w=["stgA"], dma=True)
        P.op("sp", lambda e: e.dma_start(out=stgB[0:64, :], in_=rows("norm_ffn_pre")), w=["stgB"], dma=True)
        P.op("sp", lambda e: e.dma_start(out=stgB[64:128, :], in_=rows("norm_ffn_post")), w=["stgB"], dma=True)
        P.op("sp", lambda e: e.dma_start(out=stgC[0:32, :], in_=rows("a_norm_v")), w=["stgC"], dma=True)
        P.op("sp", lambda e: e.dma_start(out=stgC[32:48, :], in_=rows("c_scale")), w=["stgC"], dma=True)
        P.op("sp", lambda e: e.dma_start(out=stgC[48:80, :], in_=rows("cond2")), w=["stgC"], dma=True)
        P.op("sp", lambda e: e.dma_start(out=stgC[80:81, :], in_=I["b_q_norm"][:, :]), w=["stgC"], dma=True)
        P.op("sp", lambda e: e.dma_start(out=stgC[81:82, :], in_=I["b_k_norm"][:, :]), w=["stgC"], dma=True)
        bm_rows = I["b_mod"].rearrange("l (c f) -> (l c) f", f=128)
        for i in range(3):
            P.op("sp", lambda e, i=i: e.dma_start(out=stgM[i][:], in_=bm_rows[i * 128:(i + 1) * 128, :]),
                 w=["stgM%d" % i], dma=True)

        def tr(dst, src, skey, dkey, pi):
            P.op("pe", lambda e: e.transpose(pst[pi][:, 0:128], src[:], k.ident[:]), r=[skey, "ident"], w=["pst%d" % pi])
            P.op("dve", lambda e: e.tensor_copy(dst, pst[pi][:, 0:128]), r=["pst%d" % pi], w=[dkey])

        tr(k.gT[:, 0:128], stgA, "stgA", "gT", 0)
        tr(k.gT[:, 128:256], stgB, "stgB", "gT", 1)
        tr(k.misc[:, :], stgC, "stgC", "misc", 0)
        for i in range(3):
            tr(bmT[:, i * 128:(i + 1) * 128], stgM[i], "stgM%d" % i, "bmT", (i + 1) % 2)
        P.op("act", lambda e: e.activation(out=scT[:].rearrange("p (c i) -> p i c", i=2),
                                           in_=k.misc[:, 48:80].rearrange("p (i c) -> p i c", i=2), func=AF.Silu),
             r=["misc"], w=["scT"])
        k.modv = k.modT[:].rearrange("p (l m c i) -> p l m c i", l=DEPTH, m=6, c=NCH)
        k.ABv = k.AB[:].rearrange("p (l s i q c) -> p l s i q c", l=DEPTH, s=2, i=2, q=3)
        k.gv = k.gT[:].rearrange("p (q l c) -> p q l c", q=4, l=DEPTH)
        for l in range(1):
            wsrc = I["w_mod"][l].rearrange("(kc p) n -> p kc n", p=128)
            for pc in range(16):
                s = (l * 16 + pc) % 2
                wmv = wm[s][:].rearrange("p (kc n) -> p kc n", n=768)
                P.op("pool", lambda e, wmv=wmv, wsrc=wsrc, pc=pc: e.dma_start(out=wmv, in_=wsrc[:, :, pc * 768:(pc + 1) * 768]),
                     w=["wm%d" % s], dma=True)
                for ch in range(6):
                    gch = pc * 6 + ch
                    pairs = [(wmv[:, kc, ch * 128:(ch + 1) * 128], scT[:, kc * 2:kc * 2 + 2]) for kc in range(16)]
                    P.op("pe", mm_group(psm[l][:, gch * 2:gch * 2 + 2], pairs), r=["wm%d" % s, "scT"], w=["psm%d" % l])
            mod_finish(k, l, psm[l], "psm%d" % l)
        ABv = k.ABv
        k.ABv = ABv

        k.conv_next = 0
        for _ in range(24):
            conv_step(k)
        P.flush()

    arena = es.enter_context(nc.sbuf_tensor("arena", [128, AR], F32))
    k.arena = arena
    k.wslot = [sb("wslot%d" % i, [128, 4096], BF16) for i in range(NWSLOT)]
    k.sq = sb("sq", [128, 4 * 512], BF16)
    k.rstd = sb("rstd", [128, 2 * 512])
    k.tmp = sb("tmp", [128, NTMP * 512])
    k.gm_ws = sb("gm_ws", [128, 1024])
    k.gm_bs = sb("gm_bs", [128, 1024])
    k.gm_wsr = sb("gm_wsr", [128, 4 * 1024], BF16)
    k.sm = sb("smallf", [128, 64])
    k.psall = es.enter_context(nc.psum_tensor("psall", [128, 4096], F32))
    k.psum = [k.psall[:, i * 512:(i + 1) * 512] for i in range(8)]
    k.ps_i = 0
    k.w_i = 0
    k.sq_i = 0
    k.tmp_i = 0

    run_layers(k, stop_after, dbg)
    P.flush(final=True)
    return nc


AR = 34304
NTMP = 6


def fview(k, off, n, T):
    return k.arena[:, off:off + n * T].rearrange("p (c t) -> p c t", t=T)


def bview(k, off, n, T):
    return k.arena[:, off:off + n * T // 2].bitcast(BF16).rearrange("p (c t) -> p c t", t=T)


def mod_finish(k, l, ps, pkey):
    P = k.P
    modv, ABv, gv = k.modv, k.ABv, k.gv
    for ci in range(2):
        P.op("dve", lambda e, ci=ci: e.tensor_tensor(
            k.modT[:, l * 192:(l + 1) * 192].rearrange("p (g i) -> p g i", i=2)[:, :, ci],
            ps[:, 0:192].rearrange("p (g i) -> p g i", i=2)[:, :, ci],
            k.bmT[:, l * 96:(l + 1) * 96], ALU.add), r=[pkey, "bmT"], w=["modT"])
    for sub in range(2):
        for ci in range(2):
            mo = 3 * sub
            P.op("dve", lambda e, sub=sub, ci=ci, mo=mo: e.scalar_tensor_tensor(
                ABv[:, l, sub, ci, 0, :], modv[:, l, mo + 1, :, ci], 1.0, gv[:, 2 * sub, l, :], ALU.add, ALU.mult),
                r=["modT", "gT"], w=["AB"])
            P.op("dve", lambda e, sub=sub, ci=ci, mo=mo: e.tensor_copy(
                ABv[:, l, sub, ci, 1, :], modv[:, l, mo, :, ci]), r=["modT"], w=["AB"])
            P.op("dve", lambda e, sub=sub, ci=ci, mo=mo: e.tensor_tensor(
                ABv[:, l, sub, ci, 2, :], modv[:, l, mo + 2, :, ci], gv[:, 2 * sub + 1, l, :], ALU.mult),
                r=["modT", "gT"], w=["AB"])


def mod_bg_step(k):
    l, gch = k.mod_l, k.mod_g
    if l >= DEPTH:
        return
    P = k.P
    sl = k.mod_i % 2
    k.mod_i += 1
    slot = k.gm_wsr[:, sl * 2048:(sl + 1) * 2048].rearrange("p (kc n) -> p kc n", n=128)
    skey = "wmb%d" % sl
    src = k.I["w_mod"][l].rearrange("(kc p) n -> p kc n", p=128)[:, :, gch * 128:(gch + 1) * 128]
    P.op("pool", lambda e: e.dma_start(out=slot, in_=src), w=[skey], dma="bg")
    pairs = [(slot[:, kc, :], k.scT[:, kc * 2:kc * 2 + 2]) for kc in range(16)]
    P.op("pe", mm_group(k.psum[6][:, gch * 2:gch * 2 + 2], pairs), r=[skey], w=["ps6"])
    k.mod_g += 1
    if k.mod_g == 96:
        mod_finish(k, l, k.psum[6], "ps6")
        k.mod_l += 1
        k.mod_g = 0


def conv_step(k):
    u = k.conv_next
    if u >= len(k.units):
        return
    k.conv_next += 1
    name, l, k0, nK, col0 = k.units[u]
    src = k.I[name][l].rearrange("(kc p) n -> p kc n", p=128)[:, k0:k0 + nK, col0:col0 + 256]
    dst = k.wb[u, :, 0:nK * 256].rearrange("p (kc n) -> p kc n", n=256)
    k.P.op("pool", lambda e: e.dma_start(out=dst, in_=src), w=[("wb", u)], dma="bg")


def next_ps(k):
    i = k.ps_i % getattr(k, "nps", 7)
    k.ps_i += 1
    return k.psum[i], "ps%d" % i


def wload(k, key, nK):
    P = k.P
    u = k.uidx[key]
    s = k.w_i % NWSLOT
    k.w_i += 1
    slot = k.wslot[s]
    if k.w_i % 3 == 0:
        conv_step(k)
    if getattr(k, "mod_active", False) and k.w_i % 2 == 0:
        mod_bg_step(k)
    while k.conv_next <= u:
        conv_step(k)
    P.op("sp", lambda e: e.dma_start(out=slot[:, 0:nK * 256], in_=k.wb[u, :, 0:nK * 256]),
         r=[("wb", u)], w=["wslot%d" % s], dma=True)
    return slot[:, 0:nK * 256].rearrange("p (kc n) -> p kc n", n=256), "wslot%d" % s


def rms_stats(k, src, skey, n, T, rs_slot, ones, bank=7):
    P = k.P
    sqv = k.sq[:].rearrange("p (s t) -> p s t", t=512)
    stb = k.psum[bank]
    bkey = "ps%d" % bank
    for c in range(n):
        s = k.sq_i % 4
        k.sq_i += 1
        P.op("act", lambda e, c=c, s=s: e.activation(out=sqv[:, s, 0:T], in_=src(c), func=AF.Square),
             r=[skey(c)], w=["sq%d" % s])
        P.op("pe", lambda e, c=c, s=s: e.matmul(stb[:, 0:T], ones[:], sqv[:, s, 0:T], start=(c == 0), stop=(c == n - 1)),
             r=["sq%d" % s], w=[bkey])
    rs = k.rstd[:, rs_slot * 512:rs_slot * 512 + T]
    if getattr(k, "lnexp", False):
        P.op("act", lambda e: e.activation(out=rs, in_=stb[:, 0:T], func=AF.Ln, bias=k.epsb[:, 0:1], scale=1.0),
             r=[bkey], w=["rstd%d" % rs_slot])
        P.op("act", lambda e: e.activation(out=rs, in_=rs, func=AF.Exp, scale=-0.5), r=["rstd%d" % rs_slot], w=["rstd%d" % rs_slot])
    else:
        P.op("act", lambda e: e.activation(out=rs, in_=stb[:, 0:T], func=AF.Sqrt, bias=k.epsb[:, 0:1], scale=1.0),
             r=[bkey], w=["rstd%d" % rs_slot])
        P.op("dve", lambda e: e.reciprocal(rs, rs), r=["rstd%d" % rs_slot], w=["rstd%d" % rs_slot])
    return rs, "rstd%d" % rs_slot


def tmp_slot(k, T):
    s = k.tmp_i % NTMP
    k.tmp_i += 1
    return k.tmp[:, s * 512:s * 512 + T], "tmp%d" % s


def prenorm_stats(k, X, xkey, T):
    return rms_stats(k, lambda c: X[:, c, 0:T], lambda c: (xkey, c), NCH, T, 0, k.onesD)


def prenorm_apply(k, X, xkey, T, l, sub, ci, out, okey, rs, rkey):
    P = k.P
    for c in range(NCH):
        t, tk = tmp_slot(k, T)
        P.op("dve", lambda e, c=c, t=t: e.tensor_tensor(t, X[:, c, 0:T], rs, ALU.mult), r=[(xkey, c), rkey], w=[tk])
        P.op("act", lambda e, c=c, t=t: e.activation(out=out(c), in_=t, func=AF.Identity,
                                                     bias=k.ABv[:, l, sub, ci, 1, c:c + 1], scale=k.ABv[:, l, sub, ci, 0, c:c + 1]),
             r=[tk], w=[okey(c)])


def prenorm(k, X, xkey, T, l, sub, ci, out, okey):
    rs, rkey = prenorm_stats(k, X, xkey, T)
    prenorm_apply(k, X, xkey, T, l, sub, ci, out, okey, rs, rkey)


def postnorm_residual(k, F, fkey, X, xkey, xoff, T, l, sub, ci, bank=7):
    P = k.P
    rs, rkey = rms_stats(k, lambda c: F[:, c, 0:T], lambda c: (fkey, c), NCH, T, 1, k.onesD, bank=bank)
    for c in range(NCH):
        t, tk = tmp_slot(k, T)
        P.op("dve", lambda e, c=c, t=t: e.scalar_tensor_tensor(t, F[:, c, 0:T], k.ABv[:, l, sub, ci, 2, c:c + 1], rs, ALU.mult, ALU.mult),
             r=[(fkey, c), rkey], w=[tk])
        P.op("pool", lambda e, c=c, t=t: e.tensor_tensor(F[:, c, 0:T], t, X[:, c, xoff:xoff + T], ALU.add),
             r=[tk, (xkey, c)], w=[(fkey, c)])


def load_x(k, src_buf, tok0, T, X, xkey, col0=0):
    k.P.op("pool", lambda e: e.dma_start(out=X[:, :, col0:col0 + T], in_=k.xscr[src_buf][:, :, tok0:tok0 + T]),
           w=[(xkey, c) for c in range(NCH)], dma=True)


def store_x(k, dst_buf, tok0, T, F, fkey):
    k.P.op("pool", lambda e: e.dma_start(out=k.xscr[dst_buf][:, :, tok0:tok0 + T], in_=F[:, :, 0:T]),
           r=[(fkey, c) for c in range(NCH)], dma=True)


def barrier(k):
    k.P.flush()
    k.P.lastw = {}
    k.P.reads = {}


def ingest_pass(k):
    P = k.P
    T = 512
    XT = [fview(k, 0, 16, T), fview(k, 8192, 16, T)]
    stage = [k.arena[:, 16384 + i * 2048:16384 + (i + 1) * 2048] for i in range(3)]
    si = 0
    for i in range(NTOK // T):
        X, xkey = XT[i % 2], "xt%d" % (i % 2)
        for j in range(4):
            t0 = i * T + j * 128
            src = k.I["xs"][t0:t0 + 128, :] if t0 < SEQ_S else k.I["xp"][t0 - SEQ_S:t0 - SEQ_S + 128, :]
            s = si % 3
            si += 1
            stg = stage[s]
            P.op("sp", lambda e, stg=stg, src=src: e.dma_start(out=stg, in_=src), w=["stage%d" % s], dma=True)
            for c4 in range(4):
                ps, pk = next_ps(k)

                def fn(e, stg=stg, ps=ps, c4=c4):
                    ins = None
                    for ii in range(4):
                        c = c4 * 4 + ii
                        ins = e.transpose(ps[:, ii * 128:(ii + 1) * 128], stg[:, c * 128:(c + 1) * 128], k.ident[:])
                    return ins
                P.op("pe", fn, r=["stage%d" % s], w=[pk])
                dst = X[:, c4 * 4:c4 * 4 + 4, j * 128:(j + 1) * 128]
                srcv = ps[:, :].rearrange("p (c t) -> p c t", t=128)
                keys = [(xkey, c4 * 4 + ii) for ii in range(4)]
                if c4 % 2 == 0:
                    P.op("dve", lambda e, dst=dst, srcv=srcv: e.tensor_copy(dst, srcv), r=[pk], w=keys)
                else:
                    P.op("act", lambda e, dst=dst, srcv=srcv: e.activation(out=dst, in_=srcv, func=AF.Copy), r=[pk], w=keys)
        store_x(k, 0, i * T, T, X, xkey)


def emit_pass(k, src_buf):
    P = k.P
    T = 512
    XT = [fview(k, 0, 16, T), fview(k, 8192, 16, T)]
    stage = [k.arena[:, 16384 + i * 2048:16384 + (i + 1) * 2048] for i in range(3)]
    si = 0
    for i in range(NTOK // T):
        X, xkey = XT[i % 2], "xt%d" % (i % 2)
        load_x(k, src_buf, i * T, T, X, xkey)
        for j in range(4):
            t0 = i * T + j * 128
            dst = k.O["ys"][t0:t0 + 128, :] if t0 < SEQ_S else k.O["yp"][t0 - SEQ_S:t0 - SEQ_S + 128, :]
            s = si % 3
            si += 1
            stg = stage[s]
            for c4 in range(4):
                ps, pk = next_ps(k)

                def fn(e, ps=ps, c4=c4, j=j, X=X):
                    ins = None
                    for ii in range(4):
                        c = c4 * 4 + ii
                        ins = e.transpose(ps[:, ii * 128:(ii + 1) * 128], X[:, c, j * 128:(j + 1) * 128], k.ident[:])
                    return ins
                P.op("pe", fn, r=[(xkey, c4 * 4 + ii) for ii in range(4)], w=[pk])
                if c4 % 2 == 0:
                    P.op("dve", lambda e, stg=stg, ps=ps, c4=c4: e.tensor_copy(stg[:, c4 * 512:(c4 + 1) * 512], ps[:, :]), r=[pk], w=["stage%d" % s])
                else:
                    P.op("act", lambda e, stg=stg, ps=ps, c4=c4: e.activation(out=stg[:, c4 * 512:(c4 + 1) * 512], in_=ps[:, :], func=AF.Copy), r=[pk], w=["stage%d" % s])
            P.op("sp", lambda e, stg=stg, dst=dst: e.dma_start(out=dst, in_=stg), r=["stage%d" % s], dma=True)


def ffn_tile(k, l, T, HT, FT, ACTT, hook_stats=None, hook_h=None):
    P = k.P

    def down(q):
        base = (q % 2) * 11
        for cp in range(8):
            wv, wk = wload(k, ("d", l, q, cp), 11)
            for i in range(2):
                c = cp * 2 + i
                ps, pk = next_ps(k)
                pairs = [(wv[:, j, i * 128:(i + 1) * 128], ACTT[:, base + j, 0:T]) for j in range(11)]
                P.op("pe", mm_group(ps[:, 0:T], pairs), r=[wk] + [("actt", base + j) for j in range(11)], w=[pk])
                if q == 0:
                    P.op("act", lambda e, c=c, ps=ps: e.activation(out=FT[:, c, 0:T], in_=ps[:, 0:T], func=AF.Copy), r=[pk], w=[("ft", c)])
                else:
                    P.op("dve", lambda e, c=c, ps=ps: e.tensor_tensor(FT[:, c, 0:T], FT[:, c, 0:T], ps[:, 0:T], ALU.add), r=[pk, ("ft", c)], w=[("ft", c)])

    qdone = 0
    for p in range(22):
        gv, gk = wload(k, ("g", l, p), 16)
        uv, uk = wload(k, ("u", l, p), 16)
        for i in range(2):
            j = 2 * p + i
            psg, pgk = next_ps(k)
            psu, puk = next_ps(k)
            hkeys = [("ht", kc) for kc in range(16)]
            P.op("pe", mm_group(psg[:, 0:T], [(gv[:, kc, i * 128:(i + 1) * 128], HT[:, kc, 0:T]) for kc in range(16)]),
                 r=[gk] + hkeys, w=[pgk])
            P.op("pe", mm_group(psu[:, 0:T], [(uv[:, kc, i * 128:(i + 1) * 128], HT[:, kc, 0:T]) for kc in range(16)]),
                 r=[uk] + hkeys, w=[puk])
            t, tk = tmp_slot(k, T)
            P.op("act", lambda e, t=t, psg=psg: e.activation(out=t, in_=psg[:, 0:T], func=AF.Silu), r=[pgk], w=[tk])
            slot = ((j // 11) % 2) * 11 + (j % 11)
            P.op("dve", lambda e, t=t, psu=psu, slot=slot: e.tensor_tensor(ACTT[:, slot, 0:T], t, psu[:, 0:T], ALU.mult),
                 r=[tk, puk], w=[("actt", slot)])
        if p == 17 and hook_stats is not None:
            hook_stats()
        if p == 21 and hook_h is not None:
            hook_h()
        while qdone < 4 and 11 * qdone + 10 <= 2 * p + 1:
            down(qdone)
            qdone += 1


def ffn_pass(k, l, src_buf, dst_buf):
    T = 512
    XT = [fview(k, 0, 16, T), fview(k, 8192, 16, T)]
    HT = bview(k, 16384, 16, T)
    ACTT = bview(k, 20480, 22, T)
    FT = fview(k, 26112, 16, T)
    ntile = NTOK // T
    hout = lambda c: HT[:, c, 0:T]
    hkey = lambda c: ("ht", c)
    if l == 0:
        k.mod_l, k.mod_g, k.mod_i = 1, 0, 0
        k.mod_active = True
        k.nps = 6
    load_x(k, src_buf, 0, T, XT[0], "xt0")
    for i in range(ntile):
        X, xkey = XT[i % 2], "xt%d" % (i % 2)
        ci = 0 if i * T < SEQ_S else 1
        if i + 1 < ntile:
            load_x(k, src_buf, (i + 1) * T, T, XT[(i + 1) % 2], "xt%d" % ((i + 1) % 2))
        prenorm(k, X, xkey, T, l, 1, ci, hout, hkey)
        ffn_tile(k, l, T, HT, FT, ACTT)
        postnorm_residual(k, FT, "ft", X, xkey, 0, T, l, 1, ci)
        store_x(k, dst_buf, i * T, T, FT, "ft")
    if l == 0:
        while k.mod_l < DEPTH:
            mod_bg_step(k)
        k.mod_active = False
        k.nps = 7


def gmlp_pass(k, l, a, src_buf, dst_buf):
    P = k.P
    T = 512
    XT = [fview(k, 0, 16, T), fview(k, 8192, 16, T)]
    HT = bview(k, 16384, 16, T)
    FT = fview(k, 20480, 16, T)
    U = bview(k, 20480, 16, T)
    V = k.arena[:, 24576:28672].bitcast(BF16).rearrange("p (j f) -> p j f", f=2048)
    wstage = fview(k, 28672, 8, 128)
    smv = k.sm
    P.op("sp", lambda e: e.dma_start(out=wstage, in_=k.I["a_w_s"][a].rearrange("g p q -> p g q")), w=["wstage"], dma=True)
    P.op("sp", lambda e: e.dma_start(out=k.gm_bs[0:1, :], in_=k.I["a_b_s"][a:a + 1].rearrange("a g p -> a (g p)")), w=["bsrow"], dma=True)
    for g in range(8):
        ps, pk = next_ps(k)
        P.op("pe", lambda e, ps=ps, g=g: e.transpose(ps[:, 0:128], wstage[:, g, :], k.ident[:]), r=["wstage"], w=[pk])
        P.op("dve", lambda e, ps=ps, g=g: e.tensor_copy(k.gm_ws[:, g * 128:(g + 1) * 128], ps[:, 0:128]), r=[pk], w=["gm_ws"])
    onesrow = k.arena[0:1, 30720:30784].bitcast(BF16)
    hi = k.arena[0:1, 30848:31360].bitcast(BF16)
    lo = k.arena[0:1, 31360:31872].bitcast(BF16)
    hi32 = k.arena[0:1, 31872:32896]
    P.op("dve", lambda e: e.memset(onesrow, 1.0), w=["onesrow"])
    P.op("dve", lambda e: e.tensor_copy(hi, k.gm_bs[0:1, :]), r=["bsrow"], w=["bshi"])
    P.op("dve", lambda e: e.tensor_copy(hi32, hi), r=["bshi"], w=["bshi32"])
    P.op("dve", lambda e: e.tensor_tensor(hi32, k.gm_bs[0:1, :], hi32, ALU.subtract), r=["bsrow", "bshi32"], w=["bshi32"])
    P.op("dve", lambda e: e.tensor_copy(lo, hi32), r=["bshi32"], w=["bslo"])
    for h in range(2):
        ps, pk = next_ps(k)
        P.op("pe", mm_group(ps[:, 0:512], [(onesrow, hi[:, h * 512:(h + 1) * 512]), (onesrow, lo[:, h * 512:(h + 1) * 512])]),
             r=["onesrow", "bshi", "bslo"], w=[pk])
        P.op("act", lambda e, ps=ps, h=h: e.activation(out=wstage_b(k)[:, h * 512:(h + 1) * 512], in_=ps[:, 0:512], func=AF.Copy), r=[pk], w=["bsb"])
    BSB = wstage_b(k)

    ntile = NTOK // T
    load_x(k, src_buf, 0, T, XT[0], "xt0")
    for i in range(ntile):
        X, xkey = XT[i % 2], "xt%d" % (i % 2)
        if i + 1 < ntile:
            load_x(k, src_buf, (i + 1) * T, T, XT[(i + 1) % 2], "xt%d" % ((i + 1) % 2))
        ci = 0 if i * T < SEQ_S else 1
        prenorm(k, X, xkey, T, l, 0, ci, lambda c: HT[:, c, 0:T], lambda c: ("ht", c))
        hkeys = [("ht", kc) for kc in range(16)]
        P.op("pool", lambda e: e.memset(smv[:, 0:32], 0.0), w=["ssq"])
        for j in range(8):
            wv, wk = wload(k, ("av", a, j), 16)
            for tc in range(4):
                ps, pk = next_ps(k)
                P.op("pe", mm_group(ps[:, 0:256], [(HT[:, kc, tc * 128:(tc + 1) * 128], wv[:, kc, :]) for kc in range(16)]),
                     r=[wk] + hkeys, w=[pk])
                t, tk = tmp_slot(k, 256)
                P.op("act", lambda e, ps=ps, tc=tc, j=j, t=t: e.activation(out=t, in_=ps[:, 0:256], func=AF.Square,
                                                                         accum_out=smv[:, tc * 8 + j:tc * 8 + j + 1]), r=[pk], w=[tk, "ssq"])
                P.op("dve", lambda e, ps=ps, tc=tc, j=j: e.tensor_copy(V[:, tc, j * 256:(j + 1) * 256], ps[:, 0:256]), r=[pk], w=[pk, ("ft", 8 + 2 * tc), ("ft", 9 + 2 * tc)])
        P.op("dve", lambda e: e.tensor_reduce(smv[:, 32:36], smv[:, 0:32].rearrange("p (t j) -> p t j", j=8), mybir.AxisListType.X, ALU.add),
             r=["ssq"], w=["rv"])
        P.op("act", lambda e: e.activation(out=smv[:, 36:40], in_=smv[:, 32:36], func=AF.Sqrt, bias=k.epsb[:, 0:1], scale=1.0 / D), r=["rv"], w=["rv2"])
        P.op("dve", lambda e: e.reciprocal(smv[:, 40:44], smv[:, 36:40]), r=["rv2"], w=["rv3"])
        for tc in range(4):
            P.op("dve", lambda e, tc=tc: e.tensor_scalar(k.gm_wsr[:, tc * 1024:(tc + 1) * 1024], k.gm_ws[:, :], smv[:, 40 + tc:41 + tc], None, ALU.mult),
                 r=["rv3", "gm_ws"], w=[("wsr", tc)])
        for j in range(8):
            wv, wk = wload(k, ("au", a, j), 16)
            for ii in range(2):
                fc = 2 * j + ii
                ps, pk = next_ps(k)
                P.op("pe", mm_group(ps[:, 0:T], [(wv[:, kc, ii * 128:(ii + 1) * 128], HT[:, kc, 0:T]) for kc in range(16)]),
                     r=[wk] + hkeys, w=[pk])
                P.op("act", lambda e, ps=ps, fc=fc: e.activation(out=U[:, fc, 0:T], in_=ps[:, 0:T], func=AF.Copy), r=[pk], w=[("ft", fc // 2)])
        for fc in range(16):
            g = fc // 2
            ps, pk = next_ps(k)

            def fn(e, ps=ps, fc=fc, g=g):
                ins = None
                for tc in range(4):
                    ins = e.matmul(ps[:, tc * 128:(tc + 1) * 128], V[:, tc, fc * 128:(fc + 1) * 128],
                                   k.gm_wsr[:, tc * 1024 + g * 128:tc * 1024 + (g + 1) * 128], start=True, stop=True)
                return ins
            P.op("pe", fn, r=[("ft", 8 + jj) for jj in range(8)] + [("wsr", tc) for tc in range(4)], w=[pk])
            t, tk = tmp_slot(k, T)
            for tc in range(4):
                P.op("dve", lambda e, ps=ps, t=t, tc=tc, fc=fc, g=g: e.scalar_tensor_tensor(
                    t[:, tc * 128:(tc + 1) * 128], ps[:, tc * 128:(tc + 1) * 128], k.misc[:, a * 16 + fc:a * 16 + fc + 1],
                    BSB[:, g * 128:(g + 1) * 128], ALU.mult, ALU.add), r=[pk, "bsb"], w=[tk])
            P.op("pool", lambda e, t=t, fc=fc: e.tensor_tensor(HT[:, fc, 0:T], t, U[:, fc, 0:T], ALU.mult),
                 r=[tk, ("ft", fc // 2)], w=[("ht", fc)])
        for j in range(8):
            wv, wk = wload(k, ("ao", a, j), 16)
            for ii in range(2):
                c = 2 * j + ii
                ps, pk = next_ps(k)
                P.op("pe", mm_group(ps[:, 0:T], [(wv[:, kc, ii * 128:(ii + 1) * 128], HT[:, kc, 0:T]) for kc in range(16)]),
                     r=[wk] + hkeys, w=[pk])
                P.op("act", lambda e, ps=ps, c=c: e.activation(out=FT[:, c, 0:T], in_=ps[:, 0:T], func=AF.Copy), r=[pk], w=[("ft", c)])
        postnorm_residual(k, FT, "ft", X, xkey, 0, T, l, 0, ci)
        store_x(k, dst_buf, i * T, T, FT, "ft")


def wstage_b(k):
    return k.arena[:, 29696:30720]


def run_layers(k, stop_after, dbg):
    P = k.P
    ingest_pass(k)
    barrier(k)
    sub = 0
    src = 0
    for l in range(DEPTH):
        for s in range(2):
            if sub > stop_after:
                break
            if s == 0:
                kind = l % 3
                if kind == 0:
                    gmlp_pass(k, l, l // 3, src, 1 - src)
                elif kind == 1:
                    attn_pass(k, l, src, 1 - src)
                else:
                    pool_pass(k, l, src, 1 - src)
            else:
                ffn_pass(k, l, src, 1 - src)
            src = 1 - src
            sub += 1
            barrier(k)
    if dbg:
        T = 512
        X = fview(k, 0, 16, T)
        dv = k.O["dbg"].rearrange("p (c t) -> p c t", t=NTOK)
        for i in range(NTOK // T):
            P.op("pool", lambda e, i=i: e.dma_start(out=X[:, :, :], in_=k.xscr[src][:, :, i * T:(i + 1) * T]), w=["dx"], dma=True)
            P.op("pool", lambda e, i=i: e.dma_start(out=dv[:, :, i * T:(i + 1) * T], in_=X[:, :, :]), r=["dx"], w=["dbgout"], dma=True)
        barrier(k)
    emit_pass(k, src)


def pool_pass(k, l, src_buf, dst_buf):
    P = k.P
    W, TO = 272, 256
    XS = [fview(k, 0, 16, W), fview(k, 4352, 16, W)]
    G = [fview(k, 8704 + i * 1088, 4, W) for i in range(4)]
    PTS = [bview(k, 13056, 16, TO), bview(k, 15104, 16, TO)]
    FT = fview(k, 17152, 16, TO)
    tiles = [(0, s_, SEQ_S, 0) for s_ in range(0, SEQ_S, TO)] + [(SEQ_S + 256 * b, 0, 256, 1) for b in range(4)]
    k.nps = 6

    def stage_a(i):
        seq0, s_, L, ci = tiles[i]
        par = i % 2
        X, xk = XS[par], "xp%d" % par
        PT, pk_ = PTS[par], "pp%d" % par
        first, last = (s_ == 0), (s_ + TO == L)
        lo, hi = s_ - 8, s_ + TO + 8
        clo, chi = max(lo, 0), min(hi, L)
        xkeys = [(xk, c) for c in range(NCH)]
        if first:
            P.op("pool", lambda e: e.memset(X[:, :, 0:8], 0.0), w=xkeys)
        if last:
            P.op("pool", lambda e: e.memset(X[:, :, 264:272], 0.0), w=xkeys)
        P.op("pool", lambda e: e.dma_start(out=X[:, :, clo - lo:chi - lo], in_=k.xscr[src_buf][:, :, seq0 + clo:seq0 + chi]), w=xkeys, dma=True)
        yield
        rs, rkey = rms_stats(k, lambda c: X[:, c, 0:W], lambda c: (xk, c), NCH, W, 0, k.onesD, bank=7)
        yield
        for g in range(4):
            w = 2 << g
            H, hk = (G[0], "G0") if g % 2 == 0 else (G[3], "G3")
            A, B_ = G[1], G[2]
            ve = "dve" if g % 2 == 0 else "pool"
            for cc in range(4):
                c = 4 * g + cc
                t, tk = tmp_slot(k, W)
                P.op("dve", lambda e, c=c, t=t: e.tensor_tensor(t, X[:, c, 0:W], rs, ALU.mult), r=[(xk, c), rkey], w=[tk])
                P.op("act", lambda e, c=c, cc=cc, t=t, H=H: e.activation(out=H[:, cc, 0:W], in_=t, func=AF.Identity,
                                                                       bias=k.ABv[:, l, 0, ci, 1, c:c + 1], scale=k.ABv[:, l, 0, ci, 0, c:c + 1]),
                     r=[tk], w=[hk])
            yield
            if first:
                P.op(ve, lambda e, H=H: e.memset(H[:, :, 0:8], 0.0), w=[hk])
            if last:
                P.op(ve, lambda e, H=H: e.memset(H[:, :, 264:272], 0.0), w=[hk])
            P.op(ve, lambda e, H=H: e.tensor_tensor(A[:, :, 1:272], H[:, :, 0:271], H[:, :, 1:272], ALU.add), r=[hk], w=["G1"])
            S, sk_ = A, "G1"
            if w >= 4:
                P.op(ve, lambda e: e.tensor_tensor(B_[:, :, 2:271], A[:, :, 1:270], A[:, :, 3:272], ALU.add), r=["G1"], w=["G2"])
                S, sk_ = B_, "G2"
            if w >= 8:
                P.op(ve, lambda e: e.tensor_tensor(A[:, :, 4:268], B_[:, :, 2:266], B_[:, :, 6:270], ALU.add), r=["G2"], w=["G1"])
                S, sk_ = A, "G1"
            if w >= 16:
                P.op(ve, lambda e: e.tensor_tensor(B_[:, :, 8:264], A[:, :, 4:260], A[:, :, 12:268], ALU.add), r=["G1"], w=["G2"])
                S, sk_ = B_, "G2"
            pkeys = [(pk_, 4 * g + cc) for cc in range(4)]
            P.op("dve", lambda e, S=S, H=H, g=g, w=w: e.scalar_tensor_tensor(PT[:, 4 * g:4 * g + 4, 0:TO], S[:, :, 8:264], 1.0 / w, H[:, :, 8:264],
                                                                            ALU.mult, ALU.subtract), r=[sk_, hk], w=pkeys)
            for (flag, c0, p0, io) in ((first, 8, 0, 0), (last, 256, 248, 8)):
                if not flag:
                    continue
                for cc in range(4):
                    t, tk = tmp_slot(k, 8)
                    P.op(ve, lambda e, S=S, cc=cc, t=t, c0=c0, io=io, g=g: e.tensor_tensor(t, S[:, cc, c0:c0 + 8], k.pinv[:, g * 16 + io:g * 16 + io + 8], ALU.mult),
                         r=[sk_], w=[tk])
                    P.op(ve, lambda e, H=H, cc=cc, t=t, c0=c0, p0=p0, g=g: e.tensor_tensor(PT[:, 4 * g + cc, p0:p0 + 8], t, H[:, cc, c0:c0 + 8], ALU.subtract),
                         r=[tk, hk], w=[(pk_, 4 * g + cc)])
            yield

    def stage_b(i):
        seq0, s_, L, ci = tiles[i]
        par = i % 2
        X, xk = XS[par], "xp%d" % par
        PT, pk_ = PTS[par], "pp%d" % par
        for g in range(4):
            pkeys = [(pk_, 4 * g + cc) for cc in range(4)]
            for j in range(2):
                wv, wk = wload(k, ("cp", g, j), 4)
                for ii in range(2):
                    c = 4 * g + 2 * j + ii
                    ps, pk = next_ps(k)
                    P.op("pe", mm_group(ps[:, 0:TO], [(wv[:, kc, ii * 128:(ii + 1) * 128], PT[:, 4 * g + kc, 0:TO]) for kc in range(4)]),
                         r=[wk] + pkeys, w=[pk])
                    P.op("act", lambda e, ps=ps, c=c: e.activation(out=FT[:, c, 0:TO], in_=ps[:, 0:TO], func=AF.Copy, scale=k.misc[:, 32 + c:33 + c]),
                         r=[pk], w=[("ft", c)])
            yield
        rs, rkey = rms_stats(k, lambda c: FT[:, c, 0:TO], lambda c: ("ft", c), NCH, TO, 1, k.onesD, bank=6)
        yield
        for half in range(2):
            for c in range(8 * half, 8 * half + 8):
                t, tk = tmp_slot(k, TO)
                P.op("dve", lambda e, c=c, t=t: e.scalar_tensor_tensor(t, FT[:, c, 0:TO], k.ABv[:, l, 0, ci, 2, c:c + 1], rs, ALU.mult, ALU.mult),
                     r=[("ft", c), rkey], w=[tk])
                P.op("pool", lambda e, c=c, t=t, X=X: e.tensor_tensor(FT[:, c, 0:TO], t, X[:, c, 8:8 + TO], ALU.add),
                     r=[tk, (xk, c)], w=[("ft", c)])
            yield
        store_x(k, dst_buf, seq0 + s_, TO, FT, "ft")

    def run(gens):
        gens = list(gens)
        while gens:
            for g_ in list(gens):
                try:
                    next(g_)
                except StopIteration:
                    gens.remove(g_)

    n = len(tiles)
    run([stage_a(0)])
    for i in range(n):
        run([stage_b(i)] + ([stage_a(i + 1)] if i + 1 < n else []))
    k.nps = 7


def attn_pass(k, l, src_buf, dst_buf):
    P = k.P
    T = 256
    X = fview(k, 0, 16, T)
    HT = bview(k, 4096, 16, T)
    QTf = k.arena[:, 6144:8192].bitcast(BF16)
    OT = bview(k, 8192, 16, T)
    MT = fview(k, 10240, 16, T)
    KT = bview(k, 14336, 4, 4608)
    VA = bview(k, 23552, 36, 512)
    CS = k.arena[:, 32768:33792]
    SKs = k.arena[:, 32768:33792].rearrange("p (t n) -> p t n", n=512)
    SVs = k.arena[:, 12288:13312].rearrange("p (t n) -> p t n", n=512)
    cstage = fview(k, 10240, 4, 512)
    PTs = [k.gm_wsr[:, i * 512:(i + 1) * 512] for i in range(8)]
    sqv = k.sq[:].rearrange("p (s t) -> p s t", t=512)
    st = {"pt": 0}
    scale = 128.0 ** -0.5
    hkeys = [("ht", kc) for kc in range(16)]

    def qk_gen(ps, pk, gcol, rope, out_bf, okeys, box=None, rbank=None, own_stats=False):
        s = k.sq_i % 4
        k.sq_i += 1
        P.op("act", lambda e: e.activation(out=sqv[:, s, :], in_=ps, func=AF.Square), r=[pk], w=["sq%d" % s])
        if own_stats:
            sb_, sbk = next_ps(k)
        else:
            sb_, sbk = k.psum[7], "ps7"
        P.op("pe", lambda e: e.matmul(sb_, k.onesH[:], sqv[:, s, :], start=True, stop=True), r=["sq%d" % s], w=[sbk])
        yield
        rs, rsk = tmp_slot(k, 512)
        P.op("act", lambda e: e.activation(out=rs, in_=sb_, func=AF.Ln, bias=k.epsb[:, 0:1], scale=1.0), r=[sbk], w=[rsk])
        yield
        P.op("act", lambda e: e.activation(out=rs, in_=rs, func=AF.Exp, scale=-0.5), r=[rsk], w=[rsk])
        yield
        qn, qk_ = tmp_slot(k, 512)
        P.op("dve", lambda e: e.scalar_tensor_tensor(qn, ps, k.misc[:, gcol:gcol + 1], rs, ALU.mult, ALU.mult), r=[pk, rsk], w=[qk_])
        if box is not None:
            box["qn"] = (qn, qk_)
        yield
        ov = out_bf
        if not rope:
            P.op("act", lambda e: e.activation(out=ov, in_=qn.rearrange("p (h t) -> p h t", t=256), func=AF.Copy), r=[qk_], w=okeys)
            return
        s2 = k.sq_i % 4
        k.sq_i += 1
        P.op("act", lambda e: e.activation(out=sqv[:, s2, :], in_=qn, func=AF.Copy), r=[qk_], w=["sq%d" % s2])
        if rbank is None:
            pr, prk = next_ps(k)
        else:
            pr, prk = k.psum[rbank], "ps%d" % rbank
        P.op("pe", lambda e: e.matmul(pr, k.rmat[:], sqv[:, s2, :], start=True, stop=True), r=["sq%d" % s2], w=[prk])
        yield
        t1, t1k = tmp_slot(k, 512)
        t2, t2k = tmp_slot(k, 512)
        P.op("dve", lambda e: e.tensor_tensor(t1, qn, CS[:, 0:512], ALU.mult), r=[qk_, "cs"], w=[t1k])
        P.op("dve", lambda e: e.tensor_tensor(t2, pr, CS[:, 512:1024], ALU.mult), r=[prk, "cs"], w=[t2k])
        yield
        P.op("pool", lambda e: e.tensor_tensor(ov, t1.rearrange("p (h t) -> p h t", t=256), t2.rearrange("p (h t) -> p h t", t=256), ALU.add),
             r=[t1k, t2k], w=okeys)

    def run_gens(gens):
        gens = [g for g in gens if g is not None]
        while gens:
            for g in list(gens):
                try:
                    next(g)
                except StopIteration:
                    gens.remove(g)

    def proj_pair(unit, which, bank=None, pre=None):
        wv, wk = pre if pre is not None else wload(k, ("qkv", unit), 16)
        if bank is None:
            ps, pk = next_ps(k)
        else:
            ps, pk = k.psum[bank], "ps%d" % bank

        def fn(e):
            ins = None
            for hh in range(2):
                for kc in range(16):
                    ins = e.matmul(ps[:, hh * 256:(hh + 1) * 256], wv[:, kc, hh * 128:(hh + 1) * 128], HT[:, kc, 0:T],
                                   start=(kc == 0), stop=(kc == 15))
            return ins
        P.op("pe", fn, r=[wk] + hkeys, w=[pk])
        return ps, pk

    def v_part(chunk0, state_b=None):
        for j in range(2):
            wv, wk = wload(k, ("qkv", 10 + j), 16)
            for tc in range(2):
                ps, pk = next_ps(k)
                P.op("pe", mm_group(ps[:, 0:256], [(HT[:, kc, tc * 128:(tc + 1) * 128], wv[:, kc, :]) for kc in range(16)]),
                     r=[wk] + hkeys, w=[pk])
                P.op("dve", lambda e, ps=ps, tc=tc, j=j: e.tensor_copy(VA[:, chunk0 + tc, j * 256:(j + 1) * 256], ps[:, 0:256]),
                     r=[pk], w=[("va", chunk0 + tc)])
                if state_b is not None:
                    P.op("dve", lambda e, ps=ps, tc=tc, j=j: e.tensor_copy(SVs[:, tc, j * 256:(j + 1) * 256], ps[:, 0:256]),
                         r=[pk], w=["svs"] + [("mt", c) for c in range(8, 12)])
        if state_b is not None:
            b = state_b
            P.op("pool", lambda e: e.dma_start(out=k.O["sv"][b * 256:(b + 1) * 256, :].rearrange("(t p) n -> p t n", p=128), in_=SVs),
                 r=["svs"], dma=True)

    def k_part(key0, rope, state_b=None):
        gens, boxes = [], []
        for jp in range(2):
            ps, pk = proj_pair(8 + jp, "k")
            box = {}
            boxes.append(box)
            gens.append(qk_gen(ps, pk, 81, rope, KT[:, 2 * jp:2 * jp + 2, key0:key0 + T], [("kt", 2 * jp), ("kt", 2 * jp + 1)], box=box,
                               own_stats=(jp == 1)))
        run_gens(gens)
        if state_b is None:
            return
        for jp in range(2):
            qn, qk_ = boxes[jp]["qn"]
            pt_, ptk = next_ps(k)

            def fn(e, qn=qn, pt_=pt_):
                ins = None
                for tc in range(2):
                    for hh in range(2):
                        ins = e.transpose(pt_[:, (tc * 2 + hh) * 128:(tc * 2 + hh + 1) * 128],
                                          qn[:, hh * 256 + tc * 128:hh * 256 + (tc + 1) * 128], k.ident[:])
                return ins
            P.op("pe", fn, r=[qk_], w=[ptk])
            P.op("act", lambda e, pt_=pt_, jp=jp: e.activation(out=SKs[:, :, jp * 256:(jp + 1) * 256],
                                                             in_=pt_.rearrange("p (t n) -> p t n", n=256), func=AF.Copy),
                 r=[ptk], w=["sks"])
        b = state_b
        P.op("pool", lambda e: e.dma_start(out=k.O["sk"][b * 256:(b + 1) * 256, :].rearrange("(t p) n -> p t n", p=128), in_=SKs),
             r=["sks"], dma=True)

    def q_part(rope, nkc, ci, X, xkey, tok0):
        def qout(jp):
            return QTf[:, jp * 512:(jp + 1) * 512].rearrange("p (h t) -> p h t", t=256)
        g0 = []
        for jp in range(2):
            ps, pk = proj_pair(jp, "q")
            g0.append(qk_gen(ps, pk, 80, rope, qout(jp), [("qt", jp)], own_stats=(jp == 1)))
        run_gens(g0)
        npair = nkc // 2
        pending = None
        ACC = k.gm_ws
        hl = k.gm_bs[:, 0:512].bitcast(BF16)
        acc1 = k.gm_bs[:, 512:1024]
        for jp in range(8):
            g = jp // 2
            psO, pOk = k.psum[4 + jp % 2], "ps%d" % (4 + jp % 2)
            psD, pDk = k.psum[7], "ps7"
            qv = QTf[:, jp * 512:(jp + 1) * 512]
            gen = None
            if jp == 0:
                prew = {2: wload(k, ("qkv", 2), 16)}
            if jp + 3 < 8:
                prew[jp + 3] = wload(k, ("qkv", jp + 3), 16)
            if jp + 2 < 8:
                ps_n, pk_n = proj_pair(jp + 2, "q", bank=6, pre=prew[jp + 2])
                gen = qk_gen(ps_n, pk_n, 80, rope, qout(jp + 2), [("qt", jp + 2)], rbank=7)

            def s_mm(p, g=g, qv=qv, jp=jp):
                b = 2 * (p % 2)

                def fn(e):
                    ins = None
                    for h in range(2):
                        kc = 2 * p + h
                        ins = e.matmul(k.psum[b + h], KT[:, g, kc * 128:(kc + 1) * 128], qv, start=True, stop=True)
                    return ins
                P.op("pe", fn, r=[("kt", g), ("qt", jp)], w=["ps%d" % b, "ps%d" % (b + 1)])
            s_mm(0)
            if pending is not None:
                pending()
                pending = None
            for p in range(npair):
                if p + 1 < npair:
                    s_mm(p + 1)
                b = 2 * (p % 2)
                pi = st["pt"] % 4
                st["pt"] += 1
                pt = k.gm_wsr[:, pi * 1024:(pi + 1) * 1024]
                for h in range(2):
                    P.op("act", lambda e, b=b, pt=pt, h=h: e.activation(out=pt[:, h * 512:(h + 1) * 512], in_=k.psum[b + h], func=AF.Exp, scale=scale),
                         r=["ps%d" % (b + h)], w=[("pt", pi)])

                def fo(e, p=p, pt=pt, psO=psO, g=g):
                    ins = None
                    for h in range(2):
                        kc = 2 * p + h
                        ins = e.matmul(psO, VA[:, kc, g * 128:(g + 1) * 128], pt[:, h * 512:(h + 1) * 512],
                                       start=(kc == 0), stop=(kc == nkc - 1))
                    return ins
                P.op("pe", fo, r=[("va", 2 * p), ("va", 2 * p + 1), ("pt", pi)], w=[pOk])
                if p == 0:
                    P.op("dve", lambda e, pt=pt: e.tensor_copy(ACC[:, :], pt), r=[("pt", pi)], w=["acc"])
                else:
                    P.op("dve", lambda e, pt=pt: e.tensor_tensor(ACC[:, :], ACC[:, :], pt, ALU.add), r=[("pt", pi), "acc"], w=["acc"])
                if gen is not None and p >= 1:
                    try:
                        next(gen)
                    except StopIteration:
                        gen = None
            run_gens([gen])
            P.op("dve", lambda e: e.tensor_tensor(acc1, ACC[:, 0:512], ACC[:, 512:1024], ALU.add), r=["acc"], w=["acc1"])
            P.op("dve", lambda e: e.tensor_copy(hl[:, 0:512], acc1), r=["acc1"], w=["hl"])
            P.op("dve", lambda e: e.tensor_tensor(hl[:, 512:1024], acc1, hl[:, 0:512], ALU.subtract), r=["acc1", "hl"], w=["hl"])

            def tail(jp=jp, psO=psO, pOk=pOk, psD=psD, pDk=pDk):
                P.op("pe", mm_group(psD, [(k.ones1[:], hl[:, 0:512]), (k.ones1[:], hl[:, 512:1024])]), r=["hl"], w=[pDk])
                rd, rdk = tmp_slot(k, 512)
                P.op("act", lambda e, rd=rd, psD=psD: e.activation(out=rd, in_=psD, func=AF.Ln), r=[pDk], w=[rdk])
                P.op("act", lambda e, rd=rd: e.activation(out=rd, in_=rd, func=AF.Exp, scale=-1.0), r=[rdk], w=[rdk])
                P.op("dve", lambda e, rd=rd, jp=jp, psO=psO: e.tensor_tensor(OT[:, 2 * jp:2 * jp + 2, :], psO.rearrange("p (h t) -> p h t", t=256),
                                                                           rd.rearrange("p (h t) -> p h t", t=256), ALU.mult),
                     r=[pOk, rdk], w=[("ot", 2 * jp), ("ot", 2 * jp + 1)])
            pending = tail
        if pending is not None:
            pending()
            pending = None
        okeys = [("ot", c) for c in range(16)]
        for j in range(8):
            wv, wk = wload(k, ("bo", j), 16)
            for ii in range(2):
                c = 2 * j + ii
                ps, pk = next_ps(k)
                P.op("pe", mm_group(ps[:, 0:T], [(wv[:, kc, ii * 128:(ii + 1) * 128], OT[:, kc, 0:T]) for kc in range(16)]),
                     r=[wk] + okeys, w=[pk])
                P.op("act", lambda e, ps=ps, c=c: e.activation(out=MT[:, c, 0:T], in_=ps[:, 0:T], func=AF.Copy), r=[pk],
                     w=[("mt", c), "svs", "cstage"])
        postnorm_residual(k, MT, "mt", X, xkey, 0, T, l, 0, ci)
        store_x(k, dst_buf, tok0, T, MT, "mt")

    xkeys = [("xt0", c) for c in range(NCH)]
    k.lnexp = True
    for b in range(4):
        tok0 = SEQ_S + 256 * b
        load_x(k, src_buf, tok0, T, X, "xt0")
        prenorm(k, X, "xt0", T, l, 0, 1, lambda c: HT[:, c, 0:T], lambda c: ("ht", c))
        k_part(0, False, state_b=b)
        v_part(0, state_b=b)
        q_part(False, 2, 1, X, "xt0", tok0)
    P.op("sp", lambda e: e.dma_start(out=cstage, in_=k.I["ck"].rearrange("(kc p) n -> p kc n", p=128)), w=["cstage"] + [("mt", c) for c in range(8)], dma=True)
    for h in range(4):
        ps, pk = next_ps(k)

        def fn(e, ps=ps, h=h):
            ins = None
            for kc in range(4):
                ins = e.transpose(ps[:, kc * 128:(kc + 1) * 128], cstage[:, kc, h * 128:(h + 1) * 128], k.ident[:])
            return ins
        P.op("pe", fn, r=["cstage"], w=[pk])
        P.op("dve", lambda e, ps=ps, h=h: e.tensor_copy(KT[:, h, 4096:4608], ps[:, :]), r=[pk], w=[("kt", h)])
    P.op("sp", lambda e: e.dma_start(out=cstage, in_=k.I["cv"].rearrange("(kc p) n -> p kc n", p=128)), w=["cstage"], dma=True)
    P.op("dve", lambda e: e.tensor_copy(VA[:, 32:36, :], cstage), r=["cstage"], w=[("va", 32 + i) for i in range(4)])
    for i in range(16):
        load_x(k, src_buf, i * T, T, X, "xt0")
        P.op("sp", lambda e, i=i: e.dma_start(out=CS, in_=k.I["cs_tab"][i]), w=["cs", "sks"], dma=True)
        prenorm(k, X, "xt0", T, l, 0, 0, lambda c: HT[:, c, 0:T], lambda c: ("ht", c))
        k_part(i * T, True)
        v_part(2 * i)
    for i in range(16):
        load_x(k, src_buf, i * T, T, X, "xt0")
        P.op("sp", lambda e, i=i: e.dma_start(out=CS, in_=k.I["cs_tab"][i]), w=["cs"], dma=True)
        prenorm(k, X, "xt0", T, l, 0, 0, lambda c: HT[:, c, 0:T], lambda c: ("ht", c))
        q_part(True, 36, 0, X, "xt0", i * T)
    k.lnexp = False


def host_consts():
    ident = np.eye(128, dtype=np.float32)
    rmat = np.zeros((128, 128), np.float32)
    for j in range(64):
        rmat[2 * j + 1, 2 * j] = -1.0
        rmat[2 * j, 2 * j + 1] = 1.0
    t = np.arange(SEQ_S)
    row = (t // 64).astype(np.float32)
    col = (t % 64).astype(np.float32)
    inv = np.power(np.float32(10000.0), -np.arange(0, 64, 2, dtype=np.float32) / np.float32(64)).astype(np.float32)
    ang = np.concatenate([row[:, None] * inv, col[:, None] * inv], axis=-1).astype(np.float32)
    cos = np.repeat(np.cos(ang), 2, axis=1).T.astype(np.float32)
    sin = np.repeat(np.sin(ang), 2, axis=1).T.astype(np.float32)
    cs = np.zeros((16, 128, 2, 2, 256), np.float32)
    for i in range(16):
        for d in range(2):
            cs[i, :, 0, d, :] = cos[:, i * 256:(i + 1) * 256]
            cs[i, :, 1, d, :] = sin[:, i * 256:(i + 1) * 256]
    cs = cs.reshape(16, 128, 1024)
    pinv = np.zeros((128, 64), np.float32)
    for wi, w in enumerate((2, 4, 8, 16)):
        for i in range(8):
            cnt_lo = (i + w // 2) - max(i - w // 2, 0)
            cnt_hi = (8 - i + w // 2) if (i + w // 2 > 8) else w
            pinv[:, wi * 16 + i] = 1.0 / cnt_lo
            pinv[:, wi * 16 + 8 + i] = 1.0 / cnt_hi
    return {"ident": ident, "rmat": rmat, "cs_tab": cs, "pool_inv": pinv}


_WNAMES = ["w_mod", "b_mod", "norm_mix_pre", "norm_mix_post", "norm_ffn_pre", "norm_ffn_post",
           "a_w_in", "a_norm_v", "a_w_s", "a_b_s", "a_w_out", "b_w_qkv", "b_q_norm", "b_k_norm", "b_w_o",
           "c_scale", "f_w_gu", "f_w_down"]


def make_in_maps(inp, ncores=8):
    consts = host_consts()
    shared = {n: np.ascontiguousarray(np.asarray(inp[n], dtype=np.float32)) for n in _WNAMES}
    shared["c_w_pool"] = np.ascontiguousarray(np.asarray(inp["c_w_pool"], dtype=np.float32).reshape(4, 512, 512))
    shared.update(consts)
    maps = []
    for i in range(ncores):
        m = dict(shared)
        m["xs"] = np.ascontiguousarray(inp["x_sample"][i])
        m["xp"] = np.ascontiguousarray(np.asarray(inp["x_prompt"][4 * i:4 * i + 4]).reshape(1024, D))
        m["ck"] = np.ascontiguousarray(np.asarray(inp["cache_k"][i, 0]).reshape(512, 512))
        m["cv"] = np.ascontiguousarray(np.asarray(inp["cache_v"][i, 0]).reshape(512, 512))
        m["cond2"] = np.ascontiguousarray(np.stack([np.asarray(inp["c"][i]), np.asarray(inp["c_ctx"])]).astype(np.float32))
        maps.append(m)
    return maps


def kernel(**inp):
    nc = build()
    maps = make_in_maps(inp)
    res = run_bass_kernel_spmd(nc, maps, core_ids=list(range(8)))
    r = res.results
    yp = np.concatenate([r[i]["yp"].reshape(4, 256, D) for i in range(8)], axis=0)
    ys = np.stack([r[i]["ys"] for i in range(8)], axis=0)
    sk = np.concatenate([r[i]["sk"].reshape(4, 1, 256, 4, 128) for i in range(8)], axis=0)
    sv = np.concatenate([r[i]["sv"].reshape(4, 1, 256, 4, 128) for i in range(8)], axis=0)
    return (yp.astype(np.float32), ys.astype(np.float32), sk.astype(np.float32), sv.astype(np.float32))
```
